# Optimizing a Trainium2 kernel written in Bass

```python
import jax, jax.numpy as jnp
from jax import lax
import numpy as np

D_MODEL = 1024
BATCH = 8
SEQ = 4096
DEPTH = 2

N_MIXERS = 2
DN_HEADS = 8
DN_HEAD_DIM = 128
DN_INNER = DN_HEADS * DN_HEAD_DIM
DN_CONV = 4
DN_CHUNK = 64
DN_IN_COLS = 4 * DN_INNER + 2 * DN_HEADS
CV_WIDTH = 31
MEM_LEN = 256
XA_HEADS = 4
XA_HEAD_DIM = D_MODEL // XA_HEADS
D_FF = 4 * D_MODEL
N_DN_LAYERS = (DEPTH + 1) // 2
N_CV_LAYERS = DEPTH // 2
RMS_EPS = 1e-6
LN_EPS = 1e-5

kernel_name = "hybrid_deltanet_conformer_xattn_trunk"


def rms_norm(x, g):
    xf = x.astype(jnp.float32)
    y = xf * lax.rsqrt(jnp.mean(xf * xf, axis=-1, keepdims=True) + RMS_EPS)
    return (y * g.astype(jnp.float32)).astype(x.dtype)


def layer_norm(x, g, b):
    xf = x.astype(jnp.float32)
    mu = jnp.mean(xf, axis=-1, keepdims=True)
    xc = xf - mu
    y = xc * lax.rsqrt(jnp.mean(xc * xc, axis=-1, keepdims=True) + LN_EPS)
    return (y * g.astype(jnp.float32) + b.astype(jnp.float32)).astype(x.dtype)


def l2_normalize(x):
    return x * lax.rsqrt(jnp.sum(x * x, axis=-1, keepdims=True) + 1e-6)


def causal_depthwise_conv(x, w):
    width = w.shape[0]
    return lax.conv_general_dilated(
        x, w[:, None, :].astype(x.dtype), window_strides=(1,), padding=[(width - 1, 0)],
        dimension_numbers=("NWC", "WIO", "NWC"), feature_group_count=x.shape[-1])


def chunk_gated_delta_rule(q, k, v, g, beta):
    B, S, H, dk = q.shape
    dv = v.shape[-1]
    n = S // DN_CHUNK

    def blocks(t):
        return t.reshape(B, n, DN_CHUNK, H, t.shape[-1]).transpose(0, 3, 1, 2, 4)

    q, k, v = blocks(q), blocks(k), blocks(v)
    g = g.reshape(B, n, DN_CHUNK, H).transpose(0, 3, 1, 2)
    beta = beta.reshape(B, n, DN_CHUNK, H).transpose(0, 3, 1, 2)
    g_cum = jnp.cumsum(g, axis=-1)

    idx = jnp.arange(DN_CHUNK)
    causal = idx[:, None] >= idx[None, :]
    strict = idx[:, None] > idx[None, :]
    diff = g_cum[..., :, None] - g_cum[..., None, :]
    decay = jnp.exp(jnp.where(causal, diff, -jnp.inf))

    k_beta = k * beta[..., None]
    lower = jnp.where(strict, jnp.einsum("bhnik,bhnjk->bhnij", k_beta, k) * decay, 0.0)
    a_mat = jnp.eye(DN_CHUNK, dtype=q.dtype) + lower
    rhs = jnp.concatenate([v * beta[..., None], k_beta * jnp.exp(g_cum)[..., None]], axis=-1)
    sol = lax.linalg.triangular_solve(a_mat, rhs, left_side=True, lower=True, unit_diagonal=True)
    u, w = sol[..., :dv], sol[..., dv:]

    attn_intra = jnp.einsum("bhnik,bhnjk->bhnij", q, k) * decay
    q_dec = q * jnp.exp(g_cum)[..., None]
    g_last = g_cum[..., -1]
    k_dec = k * jnp.exp(g_last[..., None] - g_cum)[..., None]

    xs = tuple(jnp.moveaxis(t, 2, 0) for t in (q_dec, k_dec, u, w, attn_intra, g_last))

    def step(state, inp):
        qd, kd, uc, wc, ai, gl = inp
        v_new = uc - jnp.einsum("bhck,bhkv->bhcv", wc, state)
        o = jnp.einsum("bhck,bhkv->bhcv", qd, state) + jnp.einsum("bhcj,bhjv->bhcv", ai, v_new)
        state = state * jnp.exp(gl)[..., None, None] + jnp.einsum("bhck,bhcv->bhkv", kd, v_new)
        return state, o

    s0 = jnp.zeros((B, H, dk, dv), jnp.float32)
    _, o = lax.scan(step, s0, xs)
    return o.transpose(1, 0, 3, 2, 4).reshape(B, S, H, dv)


def gated_deltanet(h, w_in, w_conv, a_log, dt_bias, out_norm, w_out):
    B, S, _ = h.shape
    proj = h @ w_in
    qkv = proj[..., :3 * DN_INNER]
    z = proj[..., 3 * DN_INNER:4 * DN_INNER]
    b_raw = proj[..., 4 * DN_INNER:4 * DN_INNER + DN_HEADS].astype(jnp.float32)
    a_raw = proj[..., 4 * DN_INNER + DN_HEADS:].astype(jnp.float32)
    qkv = jax.nn.silu(causal_depthwise_conv(qkv, w_conv)).astype(jnp.float32)
    q = qkv[..., :DN_INNER].reshape(B, S, DN_HEADS, DN_HEAD_DIM)
    k = qkv[..., DN_INNER:2 * DN_INNER].reshape(B, S, DN_HEADS, DN_HEAD_DIM)
    v = qkv[..., 2 * DN_INNER:].reshape(B, S, DN_HEADS, DN_HEAD_DIM)
    q = l2_normalize(q) * (DN_HEAD_DIM ** -0.5)
    k = l2_normalize(k)
    beta = jax.nn.sigmoid(b_raw)
    g = -jnp.exp(a_log.astype(jnp.float32)) * jax.nn.softplus(a_raw + dt_bias.astype(jnp.float32))
    o = chunk_gated_delta_rule(q, k, v, g, beta)
    o = o * lax.rsqrt(jnp.mean(o * o, axis=-1, keepdims=True) + RMS_EPS) * out_norm.astype(jnp.float32)
    o = o * jax.nn.silu(z.astype(jnp.float32).reshape(B, S, DN_HEADS, DN_HEAD_DIM))
    return o.astype(h.dtype).reshape(B, S, DN_INNER) @ w_out


def conformer_conv(h, w_pw1, b_pw1, w_dw, b_dw, ln_g, ln_b, w_pw2, b_pw2):
    u = h @ w_pw1 + b_pw1
    u = u[..., :D_MODEL] * jax.nn.sigmoid(u[..., D_MODEL:])
    c = causal_depthwise_conv(u, w_dw) + b_dw
    c = jax.nn.silu(layer_norm(c, ln_g, ln_b))
    return c @ w_pw2 + b_pw2


def memory_cross_attention(h, mem_h, w_q, w_kv, w_o):
    B, S, _ = h.shape
    M = mem_h.shape[1]
    q = (h @ w_q).reshape(B, S, XA_HEADS, XA_HEAD_DIM)
    kv = (mem_h @ w_kv).reshape(B, M, 2, XA_HEADS, XA_HEAD_DIM)
    k, v = kv[:, :, 0], kv[:, :, 1]
    s = jnp.einsum("bshd,bmhd->bhsm", q, k).astype(jnp.float32) * (XA_HEAD_DIM ** -0.5)
    p = jax.nn.softmax(s, axis=-1).astype(v.dtype)
    o = jnp.einsum("bhsm,bmhd->bshd", p, v).reshape(B, S, D_MODEL)
    return o @ w_o


def sq_relu_mlp(h, w_up, w_down):
    return jnp.square(jax.nn.relu(h @ w_up)) @ w_down


def setup_inputs(seed: int = 0) -> dict:
    key = jax.random.key(seed)
    ks = jax.random.split(key, 32)
    f32 = jnp.float32

    def nrm(k, shape, scale):
        return jax.random.normal(k, shape, f32) * scale

    def gain(k, shape):
        return 1.0 + 0.02 * jax.random.normal(k, shape, f32)

    dt = jax.random.uniform(ks[6], (N_DN_LAYERS, DN_HEADS), f32, 1e-3, 1e-1)
    return {
        "x": jax.random.normal(ks[0], (BATCH, SEQ, D_MODEL), f32),
        "mem": jax.random.normal(ks[1], (BATCH, MEM_LEN, D_MODEL), f32),
        "dn_norm": gain(ks[2], (N_DN_LAYERS, D_MODEL)),
        "dn_w_in": nrm(ks[3], (N_DN_LAYERS, D_MODEL, DN_IN_COLS), D_MODEL ** -0.5),
        "dn_w_conv": nrm(ks[4], (N_DN_LAYERS, DN_CONV, 3 * DN_INNER), DN_CONV ** -0.5),
        "dn_a_log": jnp.log(jax.random.uniform(ks[5], (N_DN_LAYERS, DN_HEADS), f32, 1.0, 16.0)),
        "dn_dt_bias": dt + jnp.log(-jnp.expm1(-dt)),
        "dn_out_norm": gain(ks[7], (N_DN_LAYERS, DN_HEAD_DIM)),
        "dn_w_out": nrm(ks[8], (N_DN_LAYERS, DN_INNER, D_MODEL), DN_INNER ** -0.5),
        "cv_norm": gain(ks[9], (N_CV_LAYERS, D_MODEL)),
        "cv_w_pw1": nrm(ks[10], (N_CV_LAYERS, D_MODEL, 2 * D_MODEL), D_MODEL ** -0.5),
        "cv_b_pw1": nrm(ks[11], (N_CV_LAYERS, 2 * D_MODEL), 0.02),
        "cv_w_dw": nrm(ks[12], (N_CV_LAYERS, CV_WIDTH, D_MODEL), CV_WIDTH ** -0.5),
        "cv_b_dw": nrm(ks[13], (N_CV_LAYERS, D_MODEL), 0.02),
        "cv_ln_g": gain(ks[14], (N_CV_LAYERS, D_MODEL)),
        "cv_ln_b": nrm(ks[15], (N_CV_LAYERS, D_MODEL), 0.02),
        "cv_w_pw2": nrm(ks[16], (N_CV_LAYERS, D_MODEL, D_MODEL), D_MODEL ** -0.5),
        "cv_b_pw2": nrm(ks[17], (N_CV_LAYERS, D_MODEL), 0.02),
        "xa_norm": gain(ks[18], (DEPTH, D_MODEL)),
        "xa_mem_norm": gain(ks[19], (DEPTH, D_MODEL)),
        "xa_w_q": nrm(ks[20], (DEPTH, D_MODEL, D_MODEL), D_MODEL ** -0.5),
        "xa_w_kv": nrm(ks[21], (DEPTH, D_MODEL, 2 * D_MODEL), D_MODEL ** -0.5),
        "xa_w_o": nrm(ks[22], (DEPTH, D_MODEL, D_MODEL), D_MODEL ** -0.5),
        "mlp_norm": gain(ks[23], (DEPTH, D_MODEL)),
        "mlp_w_up": nrm(ks[24], (DEPTH, D_MODEL, D_FF), D_MODEL ** -0.5),
        "mlp_w_down": nrm(ks[25], (DEPTH, D_FF, D_MODEL), D_FF ** -0.5),
        "final_norm": gain(ks[26], (D_MODEL,)),
    }


def reference(x, mem, dn_norm, dn_w_in, dn_w_conv, dn_a_log, dn_dt_bias, dn_out_norm, dn_w_out,
              cv_norm, cv_w_pw1, cv_b_pw1, cv_w_dw, cv_b_dw, cv_ln_g, cv_ln_b, cv_w_pw2, cv_b_pw2,
              xa_norm, xa_mem_norm, xa_w_q, xa_w_kv, xa_w_o, mlp_norm, mlp_w_up, mlp_w_down,
              final_norm):
    h = x
    for layer in range(DEPTH):
        j = layer // N_MIXERS
        if layer % N_MIXERS == 0:
            h = h + gated_deltanet(rms_norm(h, dn_norm[j]), dn_w_in[j], dn_w_conv[j], dn_a_log[j],
                                   dn_dt_bias[j], dn_out_norm[j], dn_w_out[j])
        else:
            h = h + conformer_conv(rms_norm(h, cv_norm[j]), cv_w_pw1[j], cv_b_pw1[j], cv_w_dw[j],
                                   cv_b_dw[j], cv_ln_g[j], cv_ln_b[j], cv_w_pw2[j], cv_b_pw2[j])
        h = h + memory_cross_attention(rms_norm(h, xa_norm[layer]), rms_norm(mem, xa_mem_norm[layer]),
                                       xa_w_q[layer], xa_w_kv[layer], xa_w_o[layer])
        h = h + sq_relu_mlp(rms_norm(h, mlp_norm[layer]), mlp_w_up[layer], mlp_w_down[layer])
    return rms_norm(h, final_norm)
```

```python
import numpy as np
from contextlib import ExitStack
import concourse.bass as bass
import concourse.mybir as mybir
from concourse.bass_utils import run_bass_kernel_spmd

F32 = mybir.dt.float32
BF16 = mybir.dt.bfloat16
ALU = mybir.AluOpType
AF = mybir.ActivationFunctionType

S_LEN = 4096
D = 1024
T = 512
NT = S_LEN // T
MEM = 256
AW = 57000
RMS_EPS = 1e-6
LN_EPS = 1e-5
NEG = -1.0e5
DN_STOP = 0
JIT = True
JIT_OFF = 16000

VC = {}
_o = 0
for _n, _w in [("dn_norm", 8), ("cv_norm", 8), ("xa_norm0", 8), ("xa_norm1", 8), ("mlp_norm0", 8),
               ("mlp_norm1", 8), ("dn_w_conv", 96), ("dn_out_norm", 1), ("cv_b_pw1", 16),
               ("cv_b_dw", 8), ("cv_ln_g", 8), ("cv_ln_b", 8), ("cv_b_pw2", 8), ("cv_w_dw", 248),
               ("a_log", 1), ("dt_bias", 1)]:
    VC[_n] = _o
    _o += _w
NVEC = _o

WB = {}
_b = 0
for _n, _k in [("w_in", 8), ("w_out", 2), ("xa0_q", 2), ("xa0_o", 2), ("mlp0_up", 8), ("mlp0_dn", 8),
               ("pw1", 4), ("dwc", 8), ("pw2", 2), ("xa1_q", 2), ("xa1_o", 2), ("mlp1_up", 8),
               ("mlp1_dn", 8)]:
    WB[_n] = (_b, _k)
    _b += _k
NBLK = _b


class Buf:
    __slots__ = ("w", "r", "name", "excl")

    def __init__(self, name="", init=None, excl=False):
        self.excl = excl
        self.w = None
        self.r = dict(init) if init else {}
        self.name = name


class Prog:
    CE = ["tensor", "vector", "scalar", "gpsimd"]
    ENG = ["tensor", "vector", "scalar", "gpsimd", "sync"]

    def __init__(self, nc, st):
        self.nc = nc
        self.st = st
        self.streams = {e: [] for e in self.ENG}
        self.sem = {e: st.enter_context(nc.semaphore("s_" + e)) for e in self.CE}
        self.cnt = {e: 0 for e in self.CE}
        self.seen = {e: {} for e in self.ENG}
        self.dsem = {}

    def dma_sem(self, name):
        if name not in self.dsem:
            self.dsem[name] = [self.st.enter_context(self.nc.semaphore("d_" + name)), 0]
        return name

    def _waits(self, eng, reads, writes):
        need = {}

        def add(ev):
            k, v = ev
            if need.get(k, 0) < v:
                need[k] = v

        for b in reads:
            if b.w is not None:
                add(b.w)
            if b.excl:
                for ev in b.r.items():
                    if ev[0] != eng:
                        add(ev)
        for b in writes:
            if b.w is not None:
                add(b.w)
            for ev in b.r.items():
                add(ev)
        out = []
        seen = self.seen[eng]
        for k, v in need.items():
            if k == eng and eng == "tensor":
                continue
            if seen.get(k, 0) >= v:
                continue
            seen[k] = v
            out.append((k, v))
        return out

    def _commit(self, ev, reads, writes):
        for b in writes:
            b.w = ev
            b.r = {}
        k, v = ev
        for b in reads:
            if b in writes:
                continue
            if b.r.get(k, 0) < v:
                b.r[k] = v

    mute = False

    def op(self, eng, fn, reads=(), writes=()):
        if self.mute:
            return
        waits = self._waits(eng, reads, writes)
        self.cnt[eng] += 1
        ev = (eng, self.cnt[eng])
        self._commit(ev, reads, writes)
        self.streams[eng].append((waits, fn, ev))

    def dma(self, fn, semname, reads=(), writes=(), queue="sync"):
        self.dma_sem(semname)
        waits = self._waits(queue, reads, writes)
        d = self.dsem[semname]
        d[1] += 16
        ev = (semname, d[1])
        self._commit(ev, reads, writes)
        self.streams[queue].append((waits, fn, ev))

    def collect(self, bufs):
        ev = {}
        for b in bufs:
            if b.w is not None and ev.get(b.w[0], 0) < b.w[1]:
                ev[b.w[0]] = b.w[1]
            for k, v in b.r.items():
                if ev.get(k, 0) < v:
                    ev[k] = v
        return ev

    def semh(self, k):
        return self.sem[k] if k in self.sem else self.dsem[k][0]

    def emit(self, final_waits):
        nc = self.nc
        with nc.Block() as block:
            for e in self.ENG:
                def body(engine, e=e):
                    for waits, fn, ev in self.streams[e]:
                        for k, v in waits:
                            engine.wait_ge(self.semh(k), v)
                        ins = fn(engine)
                        if ev[0] in self.sem:
                            ins.then_inc(self.sem[ev[0]], 1)
                        else:
                            ins.then_inc(self.dsem[ev[0]][0], 16)
                    if e == "sync":
                        for k, v in final_waits:
                            engine.wait_ge(self.semh(k), v)
                getattr(block, e)(body)


class TT:
    def __init__(self, ap, bufs):
        self.ap = ap
        self.b = bufs


class Builder:
    def __init__(self, nc, st, ntiles, stages, final_norm=True):
        self.nc = nc
        self.st = st
        self.P = Prog(nc, st)
        self.ntiles = ntiles
        self.stages = stages
        self.final_norm = final_norm
        self.arena = st.enter_context(nc.sbuf_tensor("arena", [128, AW], F32))
        self.off = 0
        self.ps = st.enter_context(nc.psum_tensor("ps", [128, 4096], F32))
        self.pbk = [Buf("pbank%d" % i, excl=True) for i in range(8)]
        self.brot = 0
        self.declare_io()
        self.layout()

    def alloc(self, words):
        o = self.off
        self.off += (words + 7) // 8 * 8
        assert self.off <= AW, "SBUF arena overflow %d" % self.off
        return o

    def view(self, off, words, dtype=F32, pat=None, **kw):
        ap = self.arena[:, off:off + words]
        if dtype == BF16:
            ap = ap.bitcast(BF16)
        if pat:
            ap = ap.rearrange(pat, **kw)
        return ap

    def mk(self, words, dtype=F32, pat=None, nbuf=1, name="", init=None, off=None, **kw):
        if off is None:
            off = self.alloc(words)
        ap = self.view(off, words, dtype, pat, **kw)
        return TT(ap, [Buf(name + str(i), init) for i in range(nbuf)])

    def bank(self, lo=0, hi=8):
        b = lo + self.brot % (hi - lo)
        self.brot += 1
        return self.ps[:, b * 512:(b + 1) * 512], [self.pbk[b]]

    def gb(self):
        pa, pb = self.bank()
        return lambda i: (pa[:, i * 128:(i + 1) * 128], pb)

    def bank2(self):
        b = 2 * (self.brot % 4)
        self.brot += 1
        return self.ps[:, b * 512:(b + 2) * 512], [self.pbk[b], self.pbk[b + 1]]

    def mm(self, out, lhsT, rhs, start, stop, R, W):
        self.P.op("tensor", lambda e: e.matmul(out, lhsT, rhs, start=start, stop=stop), R, W)

    def tr(self, out, in_, ident, R, W):
        self.P.op("tensor", lambda e: e.transpose(out, in_, ident), R, W)

    def act(self, out, in_, func, R, W, bias=None, scale=None, accum_out=None):
        kw = {}
        if bias is not None:
            kw["bias"] = bias
        if scale is not None:
            kw["scale"] = scale
        if accum_out is not None:
            kw["accum_out"] = accum_out
        self.P.op("scalar", lambda e: e.activation(out=out, in_=in_, func=func, **kw), R, W)

    def tt(self, eng, out, a, b, op, R, W):
        self.P.op(eng, lambda e: e.tensor_tensor(out=out, in0=a, in1=b, op=op), R, W)

    def ts(self, eng, out, a, s1, op0, R, W, s2=None, op1=None):
        if op1 is None:
            self.P.op(eng, lambda e: e.tensor_scalar(out=out, in0=a, scalar1=s1, scalar2=None, op0=op0), R, W)
        else:
            self.P.op(eng, lambda e: e.tensor_scalar(out=out, in0=a, scalar1=s1, scalar2=s2, op0=op0, op1=op1), R, W)

    def stt(self, out, a, s, b, op0, op1, R, W):
        self.P.op("vector", lambda e: e.scalar_tensor_tensor(out=out, in0=a, scalar=s, in1=b, op0=op0, op1=op1), R, W)

    def cp(self, eng, out, in_, R, W):
        if eng == "scalar":
            self.P.op("scalar", lambda e: e.activation(out=out, in_=in_, func=AF.Copy), R, W)
        else:
            self.P.op(eng, lambda e: e.tensor_copy(out=out, in_=in_), R, W)

    def memset(self, eng, ap, val, W):
        self.P.op(eng, lambda e: e.memset(ap, val), (), W)

    def aselect(self, out, pattern, cm, cmp, fill, W):
        self.P.op("gpsimd", lambda e: e.affine_select(out=out, in_=out, pattern=pattern, compare_op=cmp,
                                                      fill=fill, base=0, channel_multiplier=cm), W, W)

    def declare_io(self):
        nc = self.nc

        def inp(name, shape):
            return nc.dram_tensor(name, list(shape), F32, kind="ExternalInput").ap()

        self.x = inp("x", (S_LEN, D))
        self.mem = inp("mem", (MEM, D))
        self.vecs_d = inp("vecs", (128, NVEC))
        self.memg_d = inp("memg", (2, D))
        self.fing_d = inp("fing", (1, D))
        self.w = {}
        for n, shp in [("dn_w_in", (D, 4112)), ("dn_w_out", (D, D)), ("cv_w_pw1", (D, 2 * D)),
                       ("cv_w_pw2", (D, D)), ("xa_w_q", (2, D, D)), ("xa_w_kv", (2, D, 2 * D)),
                       ("xa_w_o", (2, D, D)), ("mlp_w_up", (2, D, 4 * D)), ("mlp_w_down", (2, 4 * D, D))]:
            self.w[n] = inp(n, shp)
        self.out = nc.dram_tensor("out", [S_LEN, D], F32, kind="ExternalOutput").ap()
        self.wsc = nc.dram_tensor("wsc", [NBLK, 128, 4096], BF16, kind="Internal").ap()
        self.wsc_b = [Buf("wsc%d" % i) for i in range(NBLK)]

    def layout(self):
        mk = self.mk
        self.ident_f = mk(128, F32, name="identf")
        self.ident_b = mk(64, BF16, name="identb")
        self.ones_b = mk(64, BF16, name="ones")
        self.ones128_b = mk(64, BF16, name="ones128")
        self.maskU = mk(128, name="maskU")
        self.maskL = mk(128, name="maskL")
        self.nm_bd_su = mk(128, name="nm1")
        self.nm_off_u = mk(128, name="nm2")
        self.nm_bd_sl = mk(128, name="nm3")
        self.nm_off_l = mk(128, name="nm4")
        self.sel = mk(512, BF16, "p (h m) -> p h m", name="sel", h=8)
        self.self32 = None
        self.resetm = mk(512, name="resetm")
        self.vecs = mk(NVEC, name="vecs")
        self.gfin = mk(1024, name="gfin")
        self.small = mk(16, name="small")
        self.wba = mk(64, BF16, "p (k n) -> p k n", name="wba", k=8)
        self.KT = [mk(1024, BF16, "p (c m) -> p c m", name="KT%d" % l, c=8) for l in range(2)]
        self.Vm = [mk(1024, BF16, "p (c n) -> p c n", name="V%d" % l, c=2) for l in range(2)]
        self.hT = mk(4096, F32, "p (c t) -> p c t", nbuf=8, name="hT", c=8)
        self.xn = mk(2048, BF16, "p (c t) -> p c t", nbuf=8, name="xn", c=8)
        self.NSLOT = 3
        self.wslot = [mk(2048, BF16, name="wslot%d" % i) for i in range(self.NSLOT)]
        self.S = mk(1024, F32, "p (h d) -> p h d", nbuf=8, name="S", h=8)
        self.Sb = mk(512, BF16, "p (h d) -> p h d", nbuf=8, name="Sb", h=8)
        self.halo = mk(72, F32, "p (c j) -> p c j", nbuf=24, name="halo", c=24)
        self.ubuf = mk(8 * 544 // 2, BF16, "p (c t) -> p c t", nbuf=8, name="ubuf", c=8)
        self.xin = [mk(1024, name="xin%d" % i) for i in range(2)]
        self.ost = self.xin
        self.nsq = [mk(256, BF16, name="nsq%d" % i) for i in range(2)]
        self.nln = mk(512, name="nln")
        self.nrs = mk(512, name="nrs")
        self.ncol = mk(8, name="ncol")
        self.shared0 = self.off
        self.shared_events = {}
        self.extra_bufs = []
        self.pre_stat = None
        self.jit_src = {}

    def phase(self, prev_bufs):
        flat = []
        for b in list(prev_bufs) + self.extra_bufs:
            flat += b.b if isinstance(b, TT) else [b]
        ev = self.P.collect(flat)
        for k, v in ev.items():
            if self.shared_events.get(k, 0) < v:
                self.shared_events[k] = v
        self.off = self.shared0
        return dict(self.shared_events)

    def chk(self, lvl):
        if DN_STOP == lvl:
            self.P.mute = True

    def vcol(self, name, j=0):
        c = VC[name] + j
        return self.vecs.ap[:, c:c + 1]

    def prologue(self):
        P = self.P
        V = self.vecs
        P.dma(lambda e: e.dma_start(out=V.ap, in_=self.vecs_d), "vecs", (), V.b)
        P.dma(lambda e: e.dma_start(out=self.gfin.ap, in_=self.fing_d[0:1, :].partition_broadcast(128)),
              "gfin", (), self.gfin.b)
        g = "gpsimd"
        idf = self.ident_f
        self.memset(g, idf.ap, 0.0, idf.b)
        self.aselect(idf.ap, [[-1, 128]], 1, ALU.not_equal, 1.0, idf.b)
        self.cp(g, self.ident_b.ap, idf.ap, idf.b, self.ident_b.b)
        self.memset(g, self.ones_b.ap, 1.0, self.ones_b.b)
        self.memset(g, self.ones128_b.ap, 128.0, self.ones128_b.b)
        mU, mL = self.maskU, self.maskL
        self.memset(g, mU.ap, 0.0, mU.b)
        self.aselect(mU.ap, [[1, 128]], -1, ALU.is_gt, NEG, mU.b)
        self.memset(g, mL.ap, 0.0, mL.b)
        self.aselect(mL.ap, [[-1, 128]], 1, ALU.is_gt, -NEG, mL.b)
        for m, kind in [(self.nm_bd_su, "bdu"), (self.nm_off_u, "offu"), (self.nm_bd_sl, "bdl"),
                        (self.nm_off_l, "offl")]:
            self.memset(g, m.ap, 0.0, m.b)
            if kind in ("bdu", "bdl"):
                self.memset(g, m.ap[0:64, 0:64], -1.0, m.b)
                self.memset(g, m.ap[64:128, 64:128], -1.0, m.b)
                if kind == "bdu":
                    self.aselect(m.ap, [[1, 128]], -1, ALU.is_gt, 0.0, m.b)
                else:
                    self.aselect(m.ap, [[-1, 128]], 1, ALU.is_gt, 0.0, m.b)
            elif kind == "offu":
                self.memset(g, m.ap[0:64, 64:128], -1.0, m.b)
            else:
                self.memset(g, m.ap[64:128, 0:64], -1.0, m.b)
        sel = self.sel
        self.memset(g, sel.ap[0:8], 0.0, sel.b)
        selflat = sel.ap[0:8].rearrange("p h m -> p (h m)")
        self.P.op(g, lambda e: e.affine_select(out=selflat, in_=selflat, pattern=[[-1, 8], [0, 128]],
                                               compare_op=ALU.not_equal, fill=1.0, base=0,
                                               channel_multiplier=1), sel.b, sel.b)
        rm = self.resetm
        self.memset(g, rm.ap[0:8], 1.0, rm.b)
        for n in range(4):
            self.memset(g, rm.ap[0:8, n * 128:n * 128 + 1], 0.0, rm.b)
        sm = self.small
        self.act(sm.ap[0:8, 0:1], V.ap[0:8, VC["a_log"]:VC["a_log"] + 1], AF.Exp, V.b, sm.b)
        self.ts("vector", sm.ap[0:8, 0:1], sm.ap[0:8, 0:1], -1.0, ALU.mult, sm.b, sm.b)
        for t_ in (self.S, self.Sb, self.halo, self.ubuf):
            self.memset(g, t_.ap, 0.0, t_.b)
        self.prologue_weights()

    def prologue_weights(self):
        P = self.P
        init = self.phase([])
        NR = 3
        st32 = [self.mk(4096, F32, name="st32_%d" % i, init=init) for i in range(NR)]
        st16 = [self.mk(2048, BF16, name="st16_%d" % i, init=init) for i in range(NR)]
        memtok = self.mk(2048, F32, "p (c f) -> p c f", name="memtok", init=init, c=2)
        memg = self.mk(1024, F32, name="memg", init=init)
        memn = self.mk(1024, BF16, "p (c f) -> p c f", name="memn", init=init, c=2)
        memnT = self.mk(1024, BF16, "p (c m) -> p c m", name="memnT", init=init, c=8)
        junk = self.mk(1024, F32, name="junk", init=init)
        self.pro_bufs = st32 + st16 + [memtok, memg, memn, memnT, junk]
        self.cast_i = 0

        def src_kn(wap, col0, ncols, kc):
            return wap.rearrange("(k p) n -> p k n", p=128)[:, :, col0:col0 + ncols]

        def cast_block(src_ap, kc, ncols):
            i = self.cast_i
            self.cast_i += 1
            a32, a16 = st32[i % NR], st16[i % NR]
            n = kc * ncols
            dst = a32.ap[:, 0:n].rearrange("p (k n) -> p k n", k=kc)
            P.dma(lambda e: e.dma_start(out=dst, in_=src_ap), "pl%d" % (i % NR), (), a32.b)
            h = n // 2
            self.cp("vector", a16.ap[:, 0:h], a32.ap[:, 0:h], a32.b, a16.b)
            self.cp("scalar", a16.ap[:, h:n], a32.ap[:, h:n], a32.b, a16.b)
            return a16

        def store_block(a16, blk):
            i = self.cast_i - 1
            P.dma(lambda e: e.dma_start(out=self.wsc[blk], in_=a16.ap), "ps%d" % (i % NR), a16.b,
                  [self.wsc_b[blk]])

        W = self.w
        st = self.stages
        if "l0mix" in st:
            a16 = cast_block(src_kn(W["dn_w_in"], 4096, 16, 8), 8, 16)
            self.cp("vector", self.wba.ap, a16.ap[:, 0:128].rearrange("p (k n) -> p k n", k=8), a16.b,
                    self.wba.b)
        for l in range(2):
            if ("l%dxa" % l) not in st:
                continue
            P.dma(lambda e, l=l: e.dma_start(out=memtok.ap, in_=self.mem.rearrange("(c p) f -> p c f", p=128)),
                  "memtok", (), memtok.b)
            P.dma(lambda e, l=l: e.dma_start(out=memg.ap, in_=self.memg_d[l:l + 1, :].partition_broadcast(128)),
                  "memg", (), memg.b)
            nc_ = self.ncol
            for mc in range(2):
                self.act(junk.ap, memtok.ap[:, mc, :], AF.Square, memtok.b, junk.b + nc_.b,
                         accum_out=nc_.ap[:, 0:1])
                self.act(nc_.ap[:, 1:2], nc_.ap[:, 0:1], AF.Ln, nc_.b, nc_.b, bias=RMS_EPS, scale=1.0 / D)
                self.act(nc_.ap[:, 2:3], nc_.ap[:, 1:2], AF.Exp, nc_.b, nc_.b, scale=-0.5)
                self.stt(memn.ap[:, mc, :], memtok.ap[:, mc, :], nc_.ap[:, 2:3], memg.ap, ALU.mult, ALU.mult,
                         memtok.b + nc_.b + memg.b, memn.b)
            for mc in range(2):
                pa, pb = self.bank()
                pv = pa.bitcast(BF16)
                for kc in range(8):
                    self.tr(pv[:, kc * 128:(kc + 1) * 128], memn.ap[:, mc, kc * 128:(kc + 1) * 128], self.ident_b.ap,
                            memn.b + self.ident_b.b, pb)
                self.cp("vector", memnT.ap[:, :, mc * 128:(mc + 1) * 128], pv.rearrange("p (k m) -> p k m", k=8),
                        pb, memnT.b)
            wkv = W["xa_w_kv"][l]
            for blk in range(4):
                a16 = cast_block(src_kn(wkv, blk * 512, 512, 8), 8, 512)
                wv = a16.ap.rearrange("p (k n) -> p k n", k=8)
                if blk < 2:
                    for oc in range(4):
                        pa, pb = self.bank()
                        for kc in range(8):
                            self.mm(pa[:, 0:256], wv[:, kc, oc * 128:(oc + 1) * 128], memnT.ap[:, kc, :],
                                    kc == 0, kc == 7, a16.b + memnT.b, pb)
                        self.cp("vector", self.KT[l].ap[:, blk * 4 + oc, :], pa[:, 0:256], pb, self.KT[l].b)
                else:
                    for mc in range(2):
                        pa, pb = self.bank()
                        for kc in range(8):
                            self.mm(pa, memnT.ap[:, kc, mc * 128:(mc + 1) * 128], wv[:, kc, :],
                                    kc == 0, kc == 7, a16.b + memnT.b, pb)
                        self.cp("vector", self.Vm[l].ap[:, mc, (blk - 2) * 512:(blk - 1) * 512], pa, pb,
                                self.Vm[l].b)
        self.jit_src = {}

        def do(name, wap, kc, ncols):
            b0, nb = WB[name]
            for j in range(nb):
                if JIT and name not in ("w_in", "w_out"):
                    self.jit_src[b0 + j] = ("w", src_kn(wap, j * ncols, ncols, kc), kc, ncols)
                    continue
                a16 = cast_block(src_kn(wap, j * ncols, ncols, kc), kc, ncols)
                store_block(a16, b0 + j)

        if "l0mix" in st:
            do("w_in", W["dn_w_in"], 8, 512)
            do("w_out", W["dn_w_out"], 8, 512)
        for l in range(2):
            if ("l%dxa" % l) in st:
                do("xa%d_q" % l, W["xa_w_q"][l], 8, 512)
                do("xa%d_o" % l, W["xa_w_o"][l], 8, 512)
            if ("l%dmlp" % l) in st:
                do("mlp%d_up" % l, W["mlp_w_up"][l], 8, 512)
                do("mlp%d_dn" % l, W["mlp_w_down"][l], 32, 128)
        if "l1mix" in st:
            do("pw1", W["cv_w_pw1"], 8, 512)
            do("pw2", W["cv_w_pw2"], 8, 512)
            b0, nb = WB["dwc"]
            for c in range(8):
                if JIT:
                    self.jit_src[b0 + c] = ("dwc", c)
                    continue
                i = self.cast_i
                self.cast_i += 1
                a16 = st16[i % NR]
                dv = a16.ap.rearrange("p (j n) -> p j n", j=32)
                for j in range(31):
                    if j % 2:
                        self.act(dv[:, j, :], self.ident_b.ap, AF.Copy, self.ident_b.b + self.vecs.b, a16.b,
                                 scale=self.vcol("cv_w_dw", j * 8 + c))
                    else:
                        self.ts("vector", dv[:, j, :], self.ident_b.ap,
                                self.vcol("cv_w_dw", j * 8 + c), ALU.mult, self.ident_b.b + self.vecs.b, a16.b)
                store_block(a16, b0 + c)

    def wstream_init(self):
        order = []
        st = self.stages
        for name, stg in [("w_in", "l0mix"), ("w_out", "l0mix"), ("xa0_q", "l0xa"), ("xa0_o", "l0xa"),
                          ("mlp0_up", "l0mlp"), ("mlp0_dn", "l0mlp"), ("pw1", "l1mix"), ("dwc", "l1mix"),
                          ("pw2", "l1mix"), ("xa1_q", "l1xa"), ("xa1_o", "l1xa"), ("mlp1_up", "l1mlp"),
                          ("mlp1_dn", "l1mlp")]:
            if stg in st:
                b0, nb = WB[name]
                order += list(range(b0, b0 + nb))
        self.wseq = order * self.ntiles
        self.w_per_tile = len(order)
        self.jit_st = None
        self.jit_pending = None
        self.jit_i = 0
        self.w_issued = 0
        self.w_consumed = 0

    def w_issue(self):
        i = self.w_issued
        if i >= len(self.wseq):
            return
        slot = self.wslot[i % self.NSLOT]
        blk = self.wseq[i]
        if i < self.w_per_tile and blk in self.jit_src:
            self.jit_issue(blk, slot, i % self.NSLOT)
        else:
            self.jit_flush()
            self.P.dma(lambda e: e.dma_start(out=slot.ap, in_=self.wsc[blk]), "w%d" % (i % self.NSLOT),
                       [self.wsc_b[blk]], slot.b)
        self.w_issued += 1

    def jit_flush(self):
        if self.jit_pending is not None:
            blk, slot, si = self.jit_pending
            self.jit_pending = None
            self.P.dma(lambda e: e.dma_start(out=self.wsc[blk], in_=slot.ap), "js%d" % si, slot.b, [self.wsc_b[blk]])

    def jit_issue(self, blk, slot, si):
        P = self.P
        if self.jit_st is None:
            flat = []
            for b in self.phase_bufs:
                flat += b.b if isinstance(b, TT) else [b]
            init = self.P.collect(flat)
            for k, v in self.shared_events.items():
                if init.get(k, 0) < v:
                    init[k] = v
            self.jit_st = [self.mk(4096, F32, name="jit32_%d" % i, init=init, off=self.shared0 + JIT_OFF + 4096 * i)
                           for i in range(2)]
            self.extra_bufs += self.jit_st
        src = self.jit_src[blk]
        j = self.jit_i
        self.jit_i += 1
        if src[0] == "w":
            _, src_ap, kc, ncols = src
            a32 = self.jit_st[j % 2]
            n = kc * ncols
            dst = a32.ap[:, 0:n].rearrange("p (k n) -> p k n", k=kc)
            P.dma(lambda e: e.dma_start(out=dst, in_=src_ap), "jl%d" % (j % 2), (), a32.b)
            self.jit_flush()
            h = n // 2
            self.cp("vector", slot.ap[:, 0:h], a32.ap[:, 0:h], a32.b, slot.b)
            self.cp("scalar", slot.ap[:, h:n], a32.ap[:, h:n], a32.b, slot.b)
        else:
            c = src[1]
            self.jit_flush()
            dv = slot.ap.rearrange("p (j n) -> p j n", j=32)
            for jj in range(31):
                if jj % 2:
                    self.act(dv[:, jj, :], self.ident_b.ap, AF.Copy, self.ident_b.b + self.vecs.b, slot.b,
                             scale=self.vcol("cv_w_dw", jj * 8 + c))
                else:
                    self.ts("vector", dv[:, jj, :], self.ident_b.ap, self.vcol("cv_w_dw", jj * 8 + c), ALU.mult,
                            self.ident_b.b + self.vecs.b, slot.b)
            self.memset("vector", dv[:, 31, :], 0.0, slot.b)
        self.jit_pending = (blk, slot, si)

    def w_acquire(self, blk):
        i = self.w_consumed
        assert self.wseq[i] == blk, (i, self.wseq[i], blk)
        while self.w_issued < min(len(self.wseq), i + self.NSLOT):
            self.w_issue()
        self.w_consumed += 1
        return self.wslot[i % self.NSLOT]

    def rmsnorm_xn(self, gname):
        hT, xn = self.hT, self.xn
        if self.pre_stat is not None:
            pa, pb = self.pre_stat
            self.pre_stat = None
        else:
            pa, pb = self.bank(4, 8)
            for c in range(8):
                sq = self.nsq[c % 2]
                self.act(sq.ap, hT.ap[:, c, :], AF.Square, [hT.b[c]], sq.b)
                self.mm(pa, self.ones_b.ap, sq.ap, c == 0, c == 7, self.ones_b.b + sq.b, pb)
        self.act(self.nln.ap, pa, AF.Ln, pb, self.nln.b, bias=RMS_EPS, scale=1.0 / D)
        self.act(self.nrs.ap, self.nln.ap, AF.Exp, self.nln.b, self.nrs.b, scale=-0.5)
        for c in range(8):
            self.stt(xn.ap[:, c, :], hT.ap[:, c, :], self.vcol(gname, c), self.nrs.ap, ALU.mult, ALU.mult,
                     [hT.b[c]] + self.nrs.b + self.vecs.b, [xn.b[c]])

    def proj(self, xin, name, evac, kc_n=8, ncols=512):
        b0, nb = WB[name]
        n_oc = ncols // 128
        for j in range(nb):
            slot = self.w_acquire(b0 + j)
            wv = slot.ap.rearrange("p (k n) -> p k n", k=kc_n)
            for oc in range(n_oc):
                pa, pb = self.bank(0, 4)
                for kc in range(kc_n):
                    self.mm(pa, wv[:, kc, oc * 128:(oc + 1) * 128], xin.ap[:, kc, :], kc == 0, kc == kc_n - 1,
                            slot.b + [xin.b[kc]], pb)
                evac(j * n_oc + oc, pa, pb)

    def resid_evac(self, bias_name=None, pre=True):
        hT = self.hT
        st_ = {}
        if pre:
            st_["bank"] = self.bank(4, 8)

        def stat(c):
            sa, sb_ = st_["bank"]
            sq = self.nsq[c % 2]
            self.mm(sa, self.ones_b.ap, sq.ap, c == 0, c == 7, self.ones_b.b + sq.b, sb_)
            if c == 7:
                self.pre_stat = (sa, sb_)

        def f(oc, pa, pb):
            if bias_name is None:
                self.tt("vector", hT.ap[:, oc, :], hT.ap[:, oc, :], pa, ALU.add, pb + [hT.b[oc]], [hT.b[oc]])
            else:
                self.stt(hT.ap[:, oc, :], pa, self.vcol(bias_name, oc), hT.ap[:, oc, :], ALU.add, ALU.add,
                         pb + [hT.b[oc]] + self.vecs.b, [hT.b[oc]])
            if pre:
                if oc > 0:
                    stat(oc - 1)
                sq = self.nsq[oc % 2]
                self.act(sq.ap, hT.ap[:, oc, :], AF.Square, [hT.b[oc]], sq.b)
                if oc == 7:
                    stat(7)
        return f

    def load_tile(self, t):
        P = self.P
        hT = self.hT
        self.pre_stat = None
        for sub in range(4):
            xi = self.xin[sub % 2]
            r0 = t * T + sub * 128
            P.dma(lambda e, xi=xi, r0=r0: e.dma_start(out=xi.ap, in_=self.x[r0:r0 + 128, :]),
                  "x%d" % (sub % 2), (), xi.b)
            pa, pb = self.bank2()
            for c in range(8):
                self.tr(pa[:, c * 128:(c + 1) * 128], xi.ap[:, c * 128:(c + 1) * 128], self.ident_f.ap,
                        xi.b + self.ident_f.b, [pb[c // 4]])
            for hf in range(2):
                src = pa[:, hf * 512:(hf + 1) * 512].rearrange("p (c t) -> p c t", c=4)
                dst = hT.ap[:, hf * 4:(hf + 1) * 4, sub * 128:(sub + 1) * 128]
                self.cp("vector" if hf else "scalar", dst, src, [pb[hf]],
                        hT.b[hf * 4:(hf + 1) * 4])

    def store_tile(self, t):
        P = self.P
        hT = self.hT
        for sub in range(4):
            os_ = self.ost[sub % 2]
            r0 = t * T + sub * 128
            pa, pb = self.bank2()
            for c in range(8):
                self.tr(pa[:, c * 128:(c + 1) * 128], hT.ap[:, c, sub * 128:(sub + 1) * 128], self.ident_f.ap,
                        [hT.b[c]] + self.ident_f.b, [pb[c // 4]])
            if self.final_norm:
                nc_ = self.ncol
                self.act(os_.ap, pa, AF.Square, pb, os_.b + nc_.b, accum_out=nc_.ap[:, 4:5])
                self.act(nc_.ap[:, 5:6], nc_.ap[:, 4:5], AF.Ln, nc_.b, nc_.b, bias=RMS_EPS, scale=1.0 / D)
                self.act(nc_.ap[:, 6:7], nc_.ap[:, 5:6], AF.Exp, nc_.b, nc_.b, scale=-0.5)
                self.stt(os_.ap, pa, nc_.ap[:, 6:7], self.gfin.ap, ALU.mult, ALU.mult,
                         pb + nc_.b + self.gfin.b, os_.b)
            else:
                self.cp("vector", os_.ap, pa, pb, os_.b)
            P.dma(lambda e, os_=os_, r0=r0: e.dma_start(out=self.out[r0:r0 + 128, :], in_=os_.ap),
                  "o%d" % (sub % 2), os_.b, ())

    def mlp(self, l):
        init = self.phase(self.phase_bufs)
        mid = self.mk(8192, BF16, "p (c t) -> p c t", nbuf=32, name="mid", init=init, c=32)
        rt = [self.mk(512, F32, name="rt%d" % i, init=init) for i in range(2)]
        self.phase_bufs = [mid] + rt
        self.rmsnorm_xn("mlp_norm%d" % l)

        def up_evac(oc, pa, pb):
            r = rt[oc % 2]
            self.act(r.ap, pa, AF.Relu, pb, r.b)
            self.tt("vector" if oc % 2 else "gpsimd", mid.ap[:, oc, :], r.ap, r.ap, ALU.mult, r.b, [mid.b[oc]])

        self.proj(self.xn, "mlp%d_up" % l, up_evac)
        self.proj(mid, "mlp%d_dn" % l, self.resid_evac(pre=(l == 0 and "l1mix" in self.stages)), kc_n=32, ncols=128)

    def xattn(self, l):
        init = self.phase(self.phase_bufs)
        qT = self.mk(2048, BF16, "p (c t) -> p c t", nbuf=8, name="xq", init=init, c=8)
        oxT = self.mk(2048, BF16, "p (c t) -> p c t", nbuf=8, name="xo", init=init, c=8)
        pT = [self.mk(512, BF16, "p (c t) -> p c t", nbuf=2, name="pT%d" % i, init=init, c=2) for i in range(2)]
        lnd = self.mk(512, F32, name="lnd", init=init)
        rden = [self.mk(512, F32, name="rden%d" % i, init=init) for i in range(2)]
        self.phase_bufs = [qT, oxT, lnd] + pT + rden
        self.rmsnorm_xn("xa_norm%d" % l)

        def q_evac(oc, pa, pb):
            self.cp("scalar" if oc % 2 else "vector", qT.ap[:, oc, :], pa, pb, [qT.b[oc]])

        self.proj(self.xn, "xa%d_q" % l, q_evac)
        KT, Vm = self.KT[l], self.Vm[l]
        def scores(a):
            pss = []
            for mc in range(2):
                pa, pb = self.bank()
                for dc in range(2):
                    self.mm(pa, KT.ap[:, 2 * a + dc, mc * 128:(mc + 1) * 128], qT.ap[:, 2 * a + dc, :],
                            dc == 0, dc == 1, KT.b + [qT.b[2 * a + dc]], pb)
                pss.append((pa, pb))
            return pss

        def rest(a, pss):
            p_ = pT[a % 2]
            rd = rden[a % 2]
            for mc in range(2):
                pa, pb = pss[mc]
                self.act(p_.ap[:, mc, :], pa, AF.Exp, pb, [p_.b[mc]], scale=1.0 / 16.0)
            pd, pdb = self.bank()
            for mc in range(2):
                self.mm(pd, self.ones_b.ap, p_.ap[:, mc, :], mc == 0, mc == 1, self.ones_b.b + [p_.b[mc]], pdb)
            pos = []
            for dvc in range(2):
                pa, pb = self.bank()
                for mc in range(2):
                    c0 = a * 256 + dvc * 128
                    self.mm(pa, Vm.ap[:, mc, c0:c0 + 128], p_.ap[:, mc, :], mc == 0, mc == 1,
                            Vm.b + [p_.b[mc]], pb)
                pos.append((pa, pb))
            self.act(lnd.ap, pd, AF.Ln, pdb, lnd.b)
            self.act(rd.ap, lnd.ap, AF.Exp, lnd.b, rd.b, scale=-1.0)
            for dvc in range(2):
                pa, pb = pos[dvc]
                self.tt("vector", oxT.ap[:, 2 * a + dvc, :], pa, rd.ap, ALU.mult, pb + rd.b, [oxT.b[2 * a + dvc]])

        prev = None
        for a in range(4):
            cur = scores(a)
            if prev is not None:
                rest(a - 1, prev)
            prev = cur
        rest(3, prev)
        self.proj(oxT, "xa%d_o" % l, self.resid_evac())

    def conformer(self):
        init = self.phase(self.phase_bufs)
        A = self.mk(4096, F32, "p (c t) -> p c t", nbuf=8, name="cvA", init=init, c=8)
        cf = self.mk(4096, F32, "p (c t) -> p c t", nbuf=8, name="cvC", init=init, c=8)
        sg = [self.mk(512, F32, name="cvsg%d" % i, init=init) for i in range(2)]
        cb = [self.mk(256, BF16, name="cvcb%d" % i, init=init) for i in range(2)]
        cq = [self.mk(256, BF16, name="cvcq%d" % i, init=init) for i in range(2)]
        mean = self.mk(512, F32, name="cvmean", init=init)
        var = self.mk(512, F32, name="cvvar", init=init)
        rstd = self.mk(512, F32, name="cvrstd", init=init)
        t1 = [self.mk(512, F32, name="cvt%d" % i, init=init) for i in range(2)]
        self.phase_bufs = [A, cf, mean, var, rstd] + sg + cb + cq + t1
        ub = self.ubuf
        self.rmsnorm_xn("cv_norm")

        def pw1_evac(oc, pa, pb):
            if oc < 8:
                self.act(A.ap[:, oc, :], pa, AF.Identity, pb + self.vecs.b, [A.b[oc]],
                         bias=self.vcol("cv_b_pw1", oc))
            else:
                c = oc - 8
                s_ = sg[c % 2]
                self.act(s_.ap, pa, AF.Sigmoid, pb + self.vecs.b, s_.b, bias=self.vcol("cv_b_pw1", oc))
                self.tt("vector" if c % 2 else "gpsimd", ub.ap[:, c, 30:542], A.ap[:, c, :], s_.ap, ALU.mult,
                        [A.b[c]] + s_.b, [ub.b[c]])

        self.proj(self.xn, "pw1", pw1_evac)
        b0, nb = WB["dwc"]
        ps1, ps1b = self.bank(4, 8)
        ps2, ps2b = self.bank(4, 8)

        def cstat(c):
            b_, q_ = cb[c % 2], cq[c % 2]
            self.mm(ps1, self.ones_b.ap, b_.ap, c == 0, c == 7, self.ones_b.b + b_.b, ps1b)
            self.mm(ps2, self.ones_b.ap, q_.ap, c == 0, c == 7, self.ones_b.b + q_.b, ps2b)

        for c in range(8):
            slot = self.w_acquire(b0 + c)
            dv = slot.ap.rearrange("p (j n) -> p j n", j=32)
            pa, pb = self.bank(0, 4)
            for j in range(31):
                self.mm(pa, dv[:, j, :], ub.ap[:, c, j:j + 512], j == 0, j == 30, slot.b + [ub.b[c]], pb)
            self.act(cf.ap[:, c, :], pa, AF.Identity, pb + self.vecs.b, [cf.b[c]], bias=self.vcol("cv_b_dw", c))
            self.cp("gpsimd", ub.ap[:, c, 0:30], ub.ap[:, c, 512:542], [ub.b[c]], [ub.b[c]])
            b_, q_ = cb[c % 2], cq[c % 2]
            self.cp("gpsimd", b_.ap, cf.ap[:, c, :], [cf.b[c]], b_.b)
            self.act(q_.ap, cf.ap[:, c, :], AF.Square, [cf.b[c]], q_.b)
            if c > 0:
                cstat(c - 1)
        cstat(7)
        self.ts("vector", mean.ap, ps1, 1.0 / D, ALU.mult, ps1b, mean.b)
        self.tt("gpsimd", var.ap, mean.ap, mean.ap, ALU.mult, mean.b, var.b)
        self.stt(var.ap, ps2, 1.0 / D, var.ap, ALU.mult, ALU.subtract, ps2b + var.b, var.b)
        self.act(var.ap, var.ap, AF.Ln, var.b, var.b, bias=LN_EPS)
        self.act(rstd.ap, var.ap, AF.Exp, var.b, rstd.b, scale=-0.5)
        xn = self.xn
        for c in range(8):
            t_ = t1[c % 2]
            self.tt("vector", t_.ap, cf.ap[:, c, :], mean.ap, ALU.subtract, [cf.b[c]] + mean.b, t_.b)
            self.tt("gpsimd" if c % 2 else "vector", t_.ap, t_.ap, rstd.ap, ALU.mult, t_.b + rstd.b, t_.b)
            self.act(xn.ap[:, c, :], t_.ap, AF.Silu, t_.b + self.vecs.b, [xn.b[c]],
                     bias=self.vcol("cv_ln_b", c), scale=self.vcol("cv_ln_g", c))
        self.proj(xn, "pw2", self.resid_evac("cv_b_pw2"))

    def deltanet(self):
        P = self.P
        init = self.phase(self.phase_bufs)
        mk = self.mk
        qT = mk(2048, BF16, "p (c t) -> p c t", nbuf=8, name="dq", init=init, c=8)
        kT = mk(2048, BF16, "p (c t) -> p c t", nbuf=8, name="dk", init=init, c=8)
        vT = mk(2048, BF16, "p (c t) -> p c t", nbuf=8, name="dv", init=init, c=8)
        zs = mk(2048, BF16, "p (c t) -> p c t", nbuf=8, name="dz", init=init, c=8)
        rows = mk(2048, F32, "p (r t) -> p r t", nbuf=4, name="rows", init=init, r=4)
        rtmp = mk(512, F32, name="rtmp", init=init)
        rtmp2 = mk(512, F32, name="rtmp2", init=init)
        rsp = mk(1280, BF16, "p (r t) -> p r t", name="rsp", init=init, r=5)
        kbT = mk(512, BF16, "p (h t) -> p h t", nbuf=8, name="kbT", init=init, h=8)
        kbgT = mk(512, BF16, "p (h t) -> p h t", nbuf=8, name="kbgT", init=init, h=8)
        qdT = mk(512, BF16, "p (h t) -> p h t", nbuf=8, name="qdT", init=init, h=8)
        oTc = mk(1024, F32, "p (h t) -> p h t", nbuf=8, name="oTc", init=init, h=8)
        osq = [mk(256, BF16, name="osq%d" % i, init=init) for i in range(2)]
        ors = mk(1024, F32, name="ors", init=init)
        ogt = oTc
        cols = mk(48, F32, name="cols", init=init)
        G = 4
        off_alias = self.off
        ctmp = [mk(516, F32, name="ctmp%d" % i, init=init) for i in range(3)]
        cacc = [mk(512, F32, name="cacc%d" % i, init=init) for i in range(3)]
        cqs = [mk(512, F32, name="cqs%d" % i, init=init) for i in range(4)]
        cth = [mk(512, F32, name="cth%d" % i, init=init) for i in range(2)]
        csq = [mk(256, BF16, name="csq%d" % i, init=init) for i in range(2)]
        cln = [mk(512, F32, name="cln%d" % i, init=init) for i in range(2)]
        end_rings = self.off
        self.off = off_alias

        def g4(words, dt, name):
            return mk(words * G, dt, "p (h t) -> p h t", nbuf=G, name=name, init=init, h=G)

        gcb = g4(128, F32, "gcb")
        E1 = g4(128, F32, "E1")
        bE = g4(128, F32, "bE")
        Eu = g4(128, F32, "Eu")
        El = g4(128, F32, "El")
        Nt = g4(128, F32, "Nt")
        attnT = g4(64, BF16, "attnT")
        Nbd = [g4(64, BF16, "Nbd%d" % i) for i in range(2)]
        Mbd = [g4(64, BF16, "Mbd%d" % i) for i in range(2)]
        Noff = g4(64, BF16, "Noff")
        Moff = g4(64, BF16, "Moff")
        Pm = g4(64, BF16, "Pm")
        Qm = g4(64, BF16, "Qm")
        Ym = g4(64, BF16, "Ym")
        TTm = g4(64, BF16, "TTm")
        kdec = g4(64, BF16, "kdec")
        vb = g4(128, F32, "vb")
        rr = g4(64, BF16, "rr")
        vnew = g4(64, BF16, "vnew")
        g4_all = [gcb, E1, bE, Eu, El, Nt, attnT, Noff, Moff, Pm, Qm, Ym, TTm, kdec, vb, rr, vnew] + Nbd + Mbd
        self.off = max(self.off, end_rings)
        self.phase_bufs = ([qT, kT, vT, zs, rows, rtmp, rtmp2, rsp, kbT, kbgT, qdT, oTc, ors, ogt, cols, gcb, E1, bE, Eu,
                            El, Nt, attnT, Noff, Moff, Pm, Qm, Ym, TTm, kdec, vb, rr, vnew]
                           + ctmp + cacc + cqs + csq + cln + cth + Nbd + Mbd + osq)
        xn, S, Sb, halo = self.xn, self.S, self.Sb, self.halo
        vecs_b = self.vecs.b

        self.rmsnorm_xn("dn_norm")

        wc = lambda oc, j: self.vcol("dn_w_conv", j * 24 + oc)
        statb = {}

        def stA(oc, pa, pb):
            tm, ac = ctmp[oc % 3], cacc[oc % 3]
            self.cp("gpsimd", tm.ap[:, 0:3], halo.ap[:, oc, :], [halo.b[oc]], tm.b)
            self.act(tm.ap[:, 3:515], pa, AF.Copy, pb, tm.b)
            self.act(ac.ap, pa, AF.Copy, pb + vecs_b, ac.b, scale=wc(oc, 3))
            self.cp("gpsimd", halo.ap[:, oc, :], tm.ap[:, 512:515], tm.b, [halo.b[oc]])
            for j in (2, 1, 0):
                self.stt(ac.ap, tm.ap[:, j:j + 512], wc(oc, j), ac.ap, ALU.mult, ALU.add, tm.b + ac.b + vecs_b, ac.b)

        def stB(oc):
            ac = cacc[oc % 3]
            if oc >= 16:
                c = oc - 16
                self.act(vT.ap[:, c, :], ac.ap, AF.Silu, ac.b, [vT.b[c]])
                return
            qs, sq = cqs[oc % 4], csq[oc % 2]
            self.act(qs.ap, ac.ap, AF.Silu, ac.b, qs.b)
            self.tt("gpsimd", sq.ap, qs.ap, qs.ap, ALU.mult, qs.b, sq.b)

        def stB2(oc):
            if not (0 <= oc < 16):
                return
            sq = csq[oc % 2]
            bi = 4 + oc % 4
            sa, sb_ = self.ps[:, bi * 512:(bi + 1) * 512], [self.pbk[bi]]
            ones = self.ones128_b if oc < 8 else self.ones_b
            self.mm(sa, ones.ap, sq.ap, True, True, ones.b + sq.b, sb_)
            statb[oc] = (sa, sb_)

        def stC(ocs):
            ocs = [oc for oc in ocs if 0 <= oc < 16]
            for oc in ocs:
                ln_ = cln[oc % 2]
                sa, sb_ = statb[oc]
                self.act(ln_.ap, sa, AF.Ln, sb_, ln_.b, bias=(128e-6 if oc < 8 else 1e-6))
            for oc in ocs:
                ln_ = cln[oc % 2]
                self.act(ln_.ap, ln_.ap, AF.Exp, ln_.b, ln_.b, scale=-0.5)
            for oc in ocs:
                qs, ln_ = cqs[oc % 4], cln[oc % 2]
                statb.pop(oc)
                isq = oc < 8
                dst = qT if isq else kT
                c = oc if isq else oc - 8
                self.tt("vector", dst.ap[:, c, :], qs.ap, ln_.ap, ALU.mult, qs.b + ln_.b, [dst.b[c]])

        def in_evac(oc, pa, pb):
            if oc >= 24:
                c = oc - 24
                self.act(zs.ap[:, c, :], pa, AF.Silu, pb, [zs.b[c]])
            else:
                stA(oc, pa, pb)
            if 0 <= oc - 1 < 24:
                stB(oc - 1)
            stB2(oc - 2)
            if (oc - 3) % 2 == 1:
                stC([oc - 4, oc - 3])

        self.proj(xn, "w_in", in_evac)
        ring_ev = self.P.collect([b for t_ in (ctmp + cacc + cqs + csq + cln + cth) for b in t_.b])
        for t_ in g4_all:
            for b in t_.b:
                for k_, v_ in ring_ev.items():
                    if b.r.get(k_, 0) < v_:
                        b.r[k_] = v_
        for z_ in (Nbd[0], Mbd[0], Noff, Moff):
            self.memset("gpsimd", z_.ap, 0.0, z_.b)

        self.chk(1)
        pB, pBb = self.bank(4, 8)
        pA, pAb = self.bank(4, 8)
        for kc in range(8):
            self.mm(pB[0:8, :], self.wba.ap[:, kc, 0:8], xn.ap[:, kc, :], kc == 0, kc == 7, self.wba.b + [xn.b[kc]], pBb)
        for kc in range(8):
            self.mm(pA[0:8, :], self.wba.ap[:, kc, 8:16], xn.ap[:, kc, :], kc == 0, kc == 7, self.wba.b + [xn.b[kc]], pAb)
        R = rows.ap
        beta_r, g_r, gc_r, gl_r = R[0:8, 0, :], R[0:8, 1, :], R[0:8, 2, :], R[0:8, 3, :]
        rt = rtmp.ap[0:8, :]
        self.chk(11)
        self.act(beta_r, pB[0:8, :], AF.Sigmoid, pBb, [rows.b[0]])
        self.chk(12)
        self.act(rt, pA[0:8, :], AF.Exp, pAb + vecs_b, rtmp.b, bias=self.vecs.ap[0:8, VC["dt_bias"]:VC["dt_bias"] + 1])
        self.act(rt, rt, AF.Ln, rtmp.b, rtmp.b, bias=1.0)
        self.chk(13)
        self.ts("vector", g_r, rt, self.small.ap[0:8, 0:1], ALU.mult, rtmp.b + self.small.b, [rows.b[1]])
        self.chk(14)
        P.op("vector", lambda e: e.tensor_tensor_scan(out=gc_r, data0=self.resetm.ap[0:8, :], data1=g_r, initial=0.0,
                                                      op0=ALU.mult, op1=ALU.add),
             self.resetm.b + [rows.b[1]], [rows.b[2]])
        self.chk(15)
        for n in range(4):
            self.ts("vector", gl_r[:, n * 128:(n + 1) * 128], gc_r[:, n * 128:(n + 1) * 128], 0.0, ALU.mult,
                    [rows.b[2]], [rows.b[3]], s2=gc_r[:, n * 128 + 127:n * 128 + 128], op1=ALU.add)

        SP_ = rsp.ap
        gch, gcm, gcl, bhi, blo = (SP_[0:8, 0, :], SP_[0:8, 1, :], SP_[0:8, 2, :], SP_[0:8, 3, :], SP_[0:8, 4, :])
        rt2 = rtmp2.ap[0:8, :]
        self.cp("vector", gch, gc_r, [rows.b[2]], rsp.b)
        self.tt("vector", rt, gc_r, gch, ALU.subtract, [rows.b[2]] + rsp.b, rtmp.b)
        self.cp("vector", gcm, rt, rtmp.b, rsp.b)
        self.tt("vector", rt2, rt, gcm, ALU.subtract, rtmp.b + rsp.b, rtmp2.b)
        self.cp("vector", gcl, rt2, rtmp2.b, rsp.b)
        self.cp("vector", bhi, beta_r, [rows.b[0]], rsp.b)
        self.tt("vector", rt2, beta_r, bhi, ALU.subtract, [rows.b[0]] + rsp.b, rtmp2.b)
        self.cp("vector", blo, rt2, rtmp2.b, rsp.b)
        self.chk(2)
        cl = cols.ap
        gc_c, be_c, gl_c, kd_c, eg_c, tp_c = (cl[:, 0:8], cl[:, 8:16], cl[:, 16:24], cl[:, 24:32], cl[:, 32:40],
                                              cl[:, 40:48])
        idb, idf = self.ident_b, self.ident_f
        for n in range(4):
            cs = slice(n * 128, (n + 1) * 128)
            pc, pcb = self.bank()
            for i, rsrc in enumerate((gc_r, beta_r, gl_r)):
                self.tr(pc[:, i * 8:(i + 1) * 8], rsrc[:, cs], idf.ap[0:8, 0:8], [rows.b[(2, 0, 3)[i]]] + idf.b, pcb)
            self.chk(21)
            self.cp("vector", cl[:, 0:24], pc[:, 0:24], pcb, cols.b)
            self.chk(22)
            self.tt("vector", tp_c, gl_c, gc_c, ALU.subtract, cols.b, cols.b)
            self.chk(23)
            self.act(kd_c, tp_c, AF.Exp, cols.b, cols.b)
            self.chk(24)
            self.act(eg_c, gl_c, AF.Exp, cols.b, cols.b)
            self.chk(3)

            def pipe(g0, hs):
                h0 = hs[0]
                i0_ = h0 - g0
                I2 = slice(i0_, i0_ + 2)
                H2 = slice(h0, h0 + 2)
                sl = lambda h: h - g0

                class GB:
                    def __init__(gself):
                        gself.pa, gself.pb = self.bank()
                        gself.pair = gself.pa[:, 0:256].rearrange("p (h t) -> p h t", h=2)
                        bfv = gself.pa.bitcast(BF16)[:, 0:512].rearrange("p (h t) -> p h t", h=2)
                        gself.pair_bf = bfv[:, :, 0:128]

                    def q(gself, h):
                        j = h - h0
                        return gself.pa[:, j * 128:(j + 1) * 128]

                    def qbf(gself, h):
                        j = h - h0
                        return gself.pa[:, j * 128:(j + 1) * 128].bitcast(BF16)[:, 0:128]

                def bl(t_, idx):
                    return [t_.b[k] for k in range(idx.start, idx.stop)]

                pg, pbt = GB(), GB()
                for h in hs:
                    for j_, part in enumerate((gch, gcm, gcl)):
                        self.mm(pg.q(h), self.sel.ap[0:8, h, :], part[:, cs], j_ == 0, j_ == 2, self.sel.b + rsp.b, pg.pb)
                    for j_, part in enumerate((bhi, blo)):
                        self.mm(pbt.q(h), self.sel.ap[0:8, h, :], part[:, cs], j_ == 0, j_ == 1, self.sel.b + rsp.b, pbt.pb)
                yield
                self.act(E1.ap[:, I2, :], pg.pair, AF.Exp, pg.pb, bl(E1, I2))
                self.cp("scalar", gcb.ap[:, I2, :], pg.pair, pg.pb, bl(gcb, I2))
                self.tt("vector", kbT.ap[:, H2, :], kT.ap[:, H2, cs], pbt.pair, ALU.mult, bl(kT, H2) + pbt.pb, bl(kbT, H2))
                self.tt("vector", bE.ap[:, I2, :], E1.ap[:, I2, :], pbt.pair, ALU.mult, bl(E1, I2) + pbt.pb, bl(bE, I2))
                self.tt("gpsimd", kbgT.ap[:, H2, :], kT.ap[:, H2, cs], bE.ap[:, I2, :], ALU.mult, bl(kT, H2) + bl(bE, I2), bl(kbgT, H2))
                self.tt("gpsimd", qdT.ap[:, H2, :], qT.ap[:, H2, cs], E1.ap[:, I2, :], ALU.mult, bl(qT, H2) + bl(E1, I2), bl(qdT, H2))
                for h in hs:
                    i = sl(h)
                    self.stt(Eu.ap[:, i, :], gcb.ap[:, i, :], gc_c[:, h:h + 1], self.maskU.ap, ALU.subtract, ALU.add,
                             [gcb.b[i]] + cols.b + self.maskU.b, [Eu.b[i]])
                    self.stt(El.ap[:, i, :], gcb.ap[:, i, :], gc_c[:, h:h + 1], self.maskL.ap, ALU.subtract, ALU.add,
                             [gcb.b[i]] + cols.b + self.maskL.b, [El.b[i]])
                self.act(Eu.ap[:, I2, :], Eu.ap[:, I2, :], AF.Exp, bl(Eu, I2), bl(Eu, I2))
                self.act(El.ap[:, I2, :], El.ap[:, I2, :], AF.Exp, bl(El, I2), bl(El, I2), scale=-1.0)
                yield
                pkkb, pkbk, pqk = GB(), GB(), GB()
                pkv_a, pkv_b = self.bank()
                for h in hs:
                    self.mm(pkkb.q(h), kT.ap[:, h, cs], kbT.ap[:, h, :], True, True, [kT.b[h], kbT.b[h]], pkkb.pb)
                    self.mm(pkbk.q(h), kbT.ap[:, h, :], kT.ap[:, h, cs], True, True, [kT.b[h], kbT.b[h]], pkbk.pb)
                    self.mm(pqk.q(h), kT.ap[:, h, cs], qT.ap[:, h, cs], True, True, [kT.b[h], qT.b[h]], pqk.pb)
                pkvb = pkv_a.bitcast(BF16)
                for h in hs:
                    j = h - h0
                    self.tr(pkvb[:, (2 * j) * 128:(2 * j + 1) * 128], kT.ap[:, h, cs], idb.ap, [kT.b[h]] + idb.b, pkv_b)
                    self.tr(pkvb[:, (2 * j + 1) * 128:(2 * j + 2) * 128], vT.ap[:, h, cs], idb.ap, [vT.b[h]] + idb.b, pkv_b)
                yield
                for h in hs:
                    i = sl(h)
                    self.tt("gpsimd", Nt.ap[:, i, :], Eu.ap[:, i, :], idf.ap, ALU.add, [Eu.b[i]] + idf.b, [Nt.b[i]])
                self.tt("vector", attnT.ap[:, I2, :], pqk.pair, Nt.ap[:, I2, :], ALU.mult, pqk.pb + bl(Nt, I2), bl(attnT, I2))
                for (r0, c0) in ((0, 0), (64, 64)):
                    rs_, cs_ = slice(r0, r0 + 64), slice(c0, c0 + 64)
                    self.stt(Nbd[0].ap[rs_, I2, cs_], pkkb.pair[rs_, :, cs_], -1.0, Eu.ap[rs_, I2, cs_], ALU.mult, ALU.mult,
                             pkkb.pb + bl(Eu, I2), bl(Nbd[0], I2))
                    self.stt(Mbd[0].ap[rs_, I2, cs_], pkbk.pair[rs_, :, cs_], -1.0, El.ap[rs_, I2, cs_], ALU.mult, ALU.mult,
                             pkbk.pb + bl(El, I2), bl(Mbd[0], I2))
                self.stt(Noff.ap[0:64, I2, 64:128], pkkb.pair[0:64, :, 64:128], -1.0, Eu.ap[0:64, I2, 64:128], ALU.mult, ALU.mult,
                         pkkb.pb + bl(Eu, I2), bl(Noff, I2))
                self.stt(Moff.ap[64:128, I2, 0:64], pkbk.pair[64:128, :, 0:64], -1.0, El.ap[64:128, I2, 0:64], ALU.mult, ALU.mult,
                         pkbk.pb + bl(El, I2), bl(Moff, I2))
                for h in hs:
                    i = sl(h)
                    j = h - h0
                    self.tt("gpsimd", Pm.ap[:, i, :], Nbd[0].ap[:, i, :], idb.ap, ALU.add, [Nbd[0].b[i]] + idb.b, [Pm.b[i]])
                    self.ts("vector", kdec.ap[:, i, :], pkvb[:, (2 * j) * 128:(2 * j + 1) * 128], kd_c[:, h:h + 1], ALU.mult,
                            pkv_b + cols.b, [kdec.b[i]])
                    self.ts("vector", vb.ap[:, i, :], pkvb[:, (2 * j + 1) * 128:(2 * j + 2) * 128], be_c[:, h:h + 1], ALU.mult,
                            pkv_b + cols.b, [vb.b[i]])
                yield
                cur = 0
                for k in range(1, 7):
                    nx = 1 - cur
                    pn = GB() if k < 5 else None
                    pm = GB() if k < 6 else None
                    pp = GB() if k > 1 else None
                    for h in hs:
                        i = sl(h)
                        if k > 1:
                            self.mm(pp.q(h), Mbd[cur].ap[:, i, :], Pm.ap[:, i, :], True, True, [Mbd[cur].b[i], Pm.b[i]], pp.pb)
                        if k < 5:
                            self.mm(pn.q(h), Mbd[cur].ap[:, i, :], Nbd[cur].ap[:, i, :], True, True,
                                    [Mbd[cur].b[i], Nbd[cur].b[i]], pn.pb)
                        if k < 6:
                            self.mm(pm.q(h), Nbd[cur].ap[:, i, :], Mbd[cur].ap[:, i, :], True, True,
                                    [Mbd[cur].b[i], Nbd[cur].b[i]], pm.pb)
                    yield
                    if k > 1:
                        self.tt("vector", Pm.ap[:, I2, :], Pm.ap[:, I2, :], pp.pair, ALU.add, bl(Pm, I2) + pp.pb, bl(Pm, I2))
                    if k < 5:
                        self.cp("scalar", Nbd[nx].ap[:, I2, :], pn.pair, pn.pb, bl(Nbd[nx], I2))
                    if k < 6:
                        self.cp("vector", Mbd[nx].ap[:, I2, :], pm.pair, pm.pb, bl(Mbd[nx], I2))
                    yield
                    cur = nx
                pqs, pys = GB(), GB()
                for h in hs:
                    i = sl(h)
                    self.tr(pqs.qbf(h), Pm.ap[:, i, :], idb.ap, [Pm.b[i]] + idb.b, pqs.pb)
                    self.mm(pys.q(h), Moff.ap[:, i, :], Pm.ap[:, i, :], True, True, [Moff.b[i], Pm.b[i]], pys.pb)
                yield
                self.cp("scalar", Qm.ap[:, I2, :], pqs.pair_bf, pqs.pb, bl(Qm, I2))
                self.cp("vector", Ym.ap[:, I2, :], pys.pair, pys.pb, bl(Ym, I2))
                yield
                pzs, p1 = GB(), GB()
                for h in hs:
                    i = sl(h)
                    self.mm(pzs.q(h), Qm.ap[:, i, :], Ym.ap[:, i, :], True, True, [Qm.b[i], Ym.b[i]], pzs.pb)
                for h in hs:
                    self.mm(p1.q(h), kbgT.ap[:, h, :], Sb.ap[:, h, :], True, True, [kbgT.b[h], Sb.b[h]], p1.pb)
                yield
                self.tt("vector", TTm.ap[:, I2, :], Pm.ap[:, I2, :], pzs.pair, ALU.add, bl(Pm, I2) + pzs.pb, bl(TTm, I2))
                self.tt("vector", rr.ap[:, I2, :], vb.ap[:, I2, :], p1.pair, ALU.subtract, bl(vb, I2) + p1.pb, bl(rr, I2))
                yield
                p2 = GB()
                for h in hs:
                    i = sl(h)
                    self.mm(p2.q(h), TTm.ap[:, i, :], rr.ap[:, i, :], True, True, [TTm.b[i], rr.b[i]], p2.pb)
                yield
                self.cp("scalar", vnew.ap[:, I2, :], p2.pair, p2.pb, bl(vnew, I2))
                yield
                p3, p4 = GB(), GB()
                for h in hs:
                    i = sl(h)
                    self.mm(p3.q(h), Sb.ap[:, h, :], qdT.ap[:, h, :], True, False, [Sb.b[h], qdT.b[h]], p3.pb)
                    self.mm(p3.q(h), vnew.ap[:, i, :], attnT.ap[:, i, :], False, True, [vnew.b[i], attnT.b[i]], p3.pb)
                    self.mm(p4.q(h), kdec.ap[:, i, :], vnew.ap[:, i, :], True, True, [kdec.b[i], vnew.b[i]], p4.pb)
                yield
                self.cp("scalar", oTc.ap[:, H2, :], p3.pair, p3.pb, bl(oTc, H2))
                for h in hs:
                    self.stt(S.ap[:, h, :], S.ap[:, h, :], eg_c[:, h:h + 1], p4.q(h), ALU.mult, ALU.add,
                             [S.b[h]] + cols.b + p4.pb, [S.b[h]])
                self.cp("gpsimd", Sb.ap[:, H2, :], S.ap[:, H2, :], bl(S, H2), bl(Sb, H2))
                yield

            for g0 in range(0, 8, G):
                gens = [pipe(g0, [g0, g0 + 1]), pipe(g0, [g0 + 2, g0 + 3])]
                live = list(gens)
                while live:
                    for gen in list(live):
                        try:
                            next(gen)
                        except StopIteration:
                            live.remove(gen)
            self.chk(9)
            sabs = []
            for hf in range(2):
                sa, sb_ = self.bank(4, 8)
                src = oTc.ap[:, hf * 4:(hf + 1) * 4, :]
                oq = osq[hf]
                self.act(oq.ap.rearrange("p (h t) -> p h t", h=4), src, AF.Square, oTc.b[hf * 4:(hf + 1) * 4], oq.b)
                self.mm(sa, self.ones_b.ap, oq.ap, True, True, self.ones_b.b + oq.b, sb_)
                sabs.append((sa, sb_))
            for hf in range(2):
                o_ = ors.ap[:, hf * 512:(hf + 1) * 512]
                self.act(o_, sabs[hf][0], AF.Ln, sabs[hf][1], ors.b, bias=RMS_EPS, scale=1.0 / 128.0)
            self.act(ors.ap, ors.ap, AF.Exp, ors.b, ors.b, scale=-0.5)
            self.stt(ogt.ap.rearrange("p h t -> p (h t)"), oTc.ap.rearrange("p h t -> p (h t)"), self.vcol("dn_out_norm"),
                     ors.ap, ALU.mult, ALU.mult, oTc.b + ors.b + vecs_b, ogt.b)
            self.tt("gpsimd", xn.ap[:, :, cs], ogt.ap, zs.ap[:, :, cs], ALU.mult, ogt.b + zs.b, xn.b)
        self.P.mute = False
        self.proj(xn, "w_out", self.resid_evac())

    def build(self):
        st = self.stages
        self.phase_bufs = []
        self.prologue()
        self.phase_bufs = self.pro_bufs
        self.wstream_init()
        for t in range(self.ntiles):
            self.load_tile(t)
            if "l0mix" in st:
                self.deltanet()
            if "l0xa" in st:
                self.xattn(0)
            if "l0mlp" in st:
                self.mlp(0)
            if "l1mix" in st:
                self.conformer()
            if "l1xa" in st:
                self.xattn(1)
            if "l1mlp" in st:
                self.mlp(1)
            self.store_tile(t)
        self.jit_flush()
        fw = [(k, v[1]) for k, v in self.P.dsem.items()]
        self.P.emit(fw)


ALL_STAGES = ("l0mix", "l0xa", "l0mlp", "l1mix", "l1xa", "l1mlp")


def build_nc(ntiles=NT, stages=ALL_STAGES, final_norm=True):
    nc = bass.Bass("TRN2", target_bir_lowering=False, dynamic_dma_scratch_size=1024)
    with ExitStack() as st:
        b = Builder(nc, st, ntiles, stages, final_norm)
        b.build()
    return nc


def pack_vecs(inp):
    v = np.zeros((128, NVEC), np.float32)

    def put(name, arr):
        a = np.asarray(arr, np.float32).reshape(-1, 128).T
        v[:, VC[name]:VC[name] + a.shape[1]] = a

    put("dn_norm", inp["dn_norm"][0])
    put("cv_norm", inp["cv_norm"][0])
    put("xa_norm0", inp["xa_norm"][0])
    put("xa_norm1", inp["xa_norm"][1])
    put("mlp_norm0", inp["mlp_norm"][0])
    put("mlp_norm1", inp["mlp_norm"][1])
    put("dn_w_conv", inp["dn_w_conv"][0].reshape(-1))
    put("dn_out_norm", inp["dn_out_norm"][0])
    put("cv_b_pw1", inp["cv_b_pw1"][0])
    put("cv_b_dw", inp["cv_b_dw"][0])
    put("cv_ln_g", inp["cv_ln_g"][0])
    put("cv_ln_b", inp["cv_ln_b"][0])
    put("cv_b_pw2", inp["cv_b_pw2"][0])
    put("cv_w_dw", inp["cv_w_dw"][0].reshape(-1))
    v[0:8, VC["a_log"]] = np.asarray(inp["dn_a_log"], np.float32)[0]
    v[0:8, VC["dt_bias"]] = np.asarray(inp["dn_dt_bias"], np.float32)[0]
    return v


def make_in_maps(inp, ncores=8):
    f = lambda a: np.ascontiguousarray(np.asarray(a, np.float32))
    shared = {
        "vecs": pack_vecs(inp),
        "memg": f(inp["xa_mem_norm"]),
        "fing": f(inp["final_norm"]).reshape(1, D),
        "dn_w_in": f(inp["dn_w_in"][0]), "dn_w_out": f(inp["dn_w_out"][0]),
        "cv_w_pw1": f(inp["cv_w_pw1"][0]), "cv_w_pw2": f(inp["cv_w_pw2"][0]),
        "xa_w_q": f(inp["xa_w_q"]), "xa_w_kv": f(inp["xa_w_kv"]), "xa_w_o": f(inp["xa_w_o"]),
        "mlp_w_up": f(inp["mlp_w_up"]), "mlp_w_down": f(inp["mlp_w_down"]),
    }
    maps = []
    for c in range(ncores):
        m = dict(shared)
        m["x"] = f(inp["x"][c])
        m["mem"] = f(inp["mem"][c])
        maps.append(m)
    return maps


def kernel(**inputs):
    nc = build_nc()
    maps = make_in_maps(inputs, 8)
    res = run_bass_kernel_spmd(nc, maps, core_ids=list(range(8)))
    return np.stack([np.asarray(r["out"], np.float32) for r in res.results], axis=0)
```

```python
import numpy as np
from contextlib import ExitStack
import concourse.bass as bass
import concourse.mybir as mybir
from concourse.bass_utils import run_bass_kernel_spmd

F32 = mybir.dt.float32
BF16 = mybir.dt.bfloat16
ALU = mybir.AluOpType
AF = mybir.ActivationFunctionType

S_LEN = 4096
D = 1024
T = 512
NT = S_LEN // T
MEM = 256
AW = 57000
RMS_EPS = 1e-6
LN_EPS = 1e-5
NEG = -1.0e5
DN_STOP = 0
JIT = True
JIT_OFF = 16000

VC = {}
_o = 0
for _n, _w in [("dn_norm", 8), ("cv_norm", 8), ("xa_norm0", 8), ("xa_norm1", 8), ("mlp_norm0", 8),
               ("mlp_norm1", 8), ("dn_w_conv", 96), ("dn_out_norm", 1), ("cv_b_pw1", 16),
               ("cv_b_dw", 8), ("cv_ln_g", 8), ("cv_ln_b", 8), ("cv_b_pw2", 8), ("cv_w_dw", 248),
               ("a_log", 1), ("dt_bias", 1)]:
    VC[_n] = _o
    _o += _w
NVEC = _o

WB = {}
_b = 0
for _n, _k in [("w_in", 8), ("w_out", 2), ("xa0_q", 2), ("xa0_o", 2), ("mlp0_up", 8), ("mlp0_dn", 8),
               ("pw1", 4), ("dwc", 8), ("pw2", 2), ("xa1_q", 2), ("xa1_o", 2), ("mlp1_up", 8),
               ("mlp1_dn", 8)]:
    WB[_n] = (_b, _k)
    _b += _k
NBLK = _b


class Buf:
    __slots__ = ("w", "r", "name", "excl")

    def __init__(self, name="", init=None, excl=False):
        self.excl = excl
        self.w = None
        self.r = dict(init) if init else {}
        self.name = name


class Prog:
    CE = ["tensor", "vector", "scalar", "gpsimd"]
    ENG = ["tensor", "vector", "scalar", "gpsimd", "sync"]

    def __init__(self, nc, st):
        self.nc = nc
        self.st = st
        self.streams = {e: [] for e in self.ENG}
        self.sem = {e: st.enter_context(nc.semaphore("s_" + e)) for e in self.CE}
        self.cnt = {e: 0 for e in self.CE}
        self.seen = {e: {} for e in self.ENG}
        self.dsem = {}

    def dma_sem(self, name):
        if name not in self.dsem:
            self.dsem[name] = [self.st.enter_context(self.nc.semaphore("d_" + name)), 0]
        return name

    def _waits(self, eng, reads, writes):
        need = {}

        def add(ev):
            k, v = ev
            if need.get(k, 0) < v:
                need[k] = v

        for b in reads:
            if b.w is not None:
                add(b.w)
            if b.excl:
                for ev in b.r.items():
                    if ev[0] != eng:
                        add(ev)
        for b in writes:
            if b.w is not None:
                add(b.w)
            for ev in b.r.items():
                add(ev)
        out = []
        seen = self.seen[eng]
        for k, v in need.items():
            if k == eng and eng == "tensor":
                continue
            if seen.get(k, 0) >= v:
                continue
            seen[k] = v
            out.append((k, v))
        return out

    def _commit(self, ev, reads, writes):
        for b in writes:
            b.w = ev
            b.r = {}
        k, v = ev
        for b in reads:
            if b in writes:
                continue
            if b.r.get(k, 0) < v:
                b.r[k] = v

    mute = False

    def op(self, eng, fn, reads=(), writes=()):
        if self.mute:
            return
        waits = self._waits(eng, reads, writes)
        self.cnt[eng] += 1
        ev = (eng, self.cnt[eng])
        self._commit(ev, reads, writes)
        self.streams[eng].append((waits, fn, ev))

    def dma(self, fn, semname, reads=(), writes=(), queue="sync"):
        self.dma_sem(semname)
        waits = self._waits(queue, reads, writes)
        d = self.dsem[semname]
        d[1] += 16
        ev = (semname, d[1])
        self._commit(ev, reads, writes)
        self.streams[queue].append((waits, fn, ev))

    def collect(self, bufs):
        ev = {}
        for b in bufs:
            if b.w is not None and ev.get(b.w[0], 0) < b.w[1]:
                ev[b.w[0]] = b.w[1]
            for k, v in b.r.items():
                if ev.get(k, 0) < v:
                    ev[k] = v
        return ev

    def semh(self, k):
        return self.sem[k] if k in self.sem else self.dsem[k][0]

    def emit(self, final_waits):
        nc = self.nc
        with nc.Block() as block:
            for e in self.ENG:
                def body(engine, e=e):
                    for waits, fn, ev in self.streams[e]:
                        for k, v in waits:
                            engine.wait_ge(self.semh(k), v)
                        ins = fn(engine)
                        if ev[0] in self.sem:
                            ins.then_inc(self.sem[ev[0]], 1)
                        else:
                            ins.then_inc(self.dsem[ev[0]][0], 16)
                    if e == "sync":
                        for k, v in final_waits:
                            engine.wait_ge(self.semh(k), v)
                getattr(block, e)(body)


class TT:
    def __init__(self, ap, bufs):
        self.ap = ap
        self.b = bufs


class Builder:
    def __init__(self, nc, st, ntiles, stages, final_norm=True):
        self.nc = nc
        self.st = st
        self.P = Prog(nc, st)
        self.ntiles = ntiles
        self.stages = stages
        self.final_norm = final_norm
        self.arena = st.enter_context(nc.sbuf_tensor("arena", [128, AW], F32))
        self.off = 0
        self.ps = st.enter_context(nc.psum_tensor("ps", [128, 4096], F32))
        self.pbk = [Buf("pbank%d" % i, excl=True) for i in range(8)]
        self.brot = 0
        self.declare_io()
        self.layout()

    def alloc(self, words):
        o = self.off
        self.off += (words + 7) // 8 * 8
        assert self.off <= AW, "SBUF arena overflow %d" % self.off
        return o

    def view(self, off, words, dtype=F32, pat=None, **kw):
        ap = self.arena[:, off:off + words]
        if dtype == BF16:
            ap = ap.bitcast(BF16)
        if pat:
            ap = ap.rearrange(pat, **kw)
        return ap

    def mk(self, words, dtype=F32, pat=None, nbuf=1, name="", init=None, off=None, **kw):
        if off is None:
            off = self.alloc(words)
        ap = self.view(off, words, dtype, pat, **kw)
        return TT(ap, [Buf(name + str(i), init) for i in range(nbuf)])

    def bank(self, lo=0, hi=8):
        b = lo + self.brot % (hi - lo)
        self.brot += 1
        return self.ps[:, b * 512:(b + 1) * 512], [self.pbk[b]]

    def gb(self):
        pa, pb = self.bank()
        return lambda i: (pa[:, i * 128:(i + 1) * 128], pb)

    def bank2(self):
        b = 2 * (self.brot % 4)
        self.brot += 1
        return self.ps[:, b * 512:(b + 2) * 512], [self.pbk[b], self.pbk[b + 1]]

    def mm(self, out, lhsT, rhs, start, stop, R, W):
        self.P.op("tensor", lambda e: e.matmul(out, lhsT, rhs, start=start, stop=stop), R, W)

    def tr(self, out, in_, ident, R, W):
        self.P.op("tensor", lambda e: e.transpose(out, in_, ident), R, W)

    def act(self, out, in_, func, R, W, bias=None, scale=None, accum_out=None):
        kw = {}
        if bias is not None:
            kw["bias"] = bias
        if scale is not None:
            kw["scale"] = scale
        if accum_out is not None:
            kw["accum_out"] = accum_out
        self.P.op("scalar", lambda e: e.activation(out=out, in_=in_, func=func, **kw), R, W)

    def tt(self, eng, out, a, b, op, R, W):
        self.P.op(eng, lambda e: e.tensor_tensor(out=out, in0=a, in1=b, op=op), R, W)

    def ts(self, eng, out, a, s1, op0, R, W, s2=None, op1=None):
        if op1 is None:
            self.P.op(eng, lambda e: e.tensor_scalar(out=out, in0=a, scalar1=s1, scalar2=None, op0=op0), R, W)
        else:
            self.P.op(eng, lambda e: e.tensor_scalar(out=out, in0=a, scalar1=s1, scalar2=s2, op0=op0, op1=op1), R, W)

    def stt(self, out, a, s, b, op0, op1, R, W):
        self.P.op("vector", lambda e: e.scalar_tensor_tensor(out=out, in0=a, scalar=s, in1=b, op0=op0, op1=op1), R, W)

    def cp(self, eng, out, in_, R, W):
        if eng == "scalar":
            self.P.op("scalar", lambda e: e.activation(out=out, in_=in_, func=AF.Copy), R, W)
        else:
            self.P.op(eng, lambda e: e.tensor_copy(out=out, in_=in_), R, W)

    def memset(self, eng, ap, val, W):
        self.P.op(eng, lambda e: e.memset(ap, val), (), W)

    def aselect(self, out, pattern, cm, cmp, fill, W):
        self.P.op("gpsimd", lambda e: e.affine_select(out=out, in_=out, pattern=pattern, compare_op=cmp,
                                                      fill=fill, base=0, channel_multiplier=cm), W, W)

    def declare_io(self):
        nc = self.nc

        def inp(name, shape):
            return nc.dram_tensor(name, list(shape), F32, kind="ExternalInput").ap()

        self.x = inp("x", (S_LEN, D))
        self.mem = inp("mem", (MEM, D))
        self.vecs_d = inp("vecs", (128, NVEC))
        self.memg_d = inp("memg", (2, D))
        self.fing_d = inp("fing", (1, D))
        self.w = {}
        for n, shp in [("dn_w_in", (D, 4112)), ("dn_w_out", (D, D)), ("cv_w_pw1", (D, 2 * D)),
                       ("cv_w_pw2", (D, D)), ("xa_w_q", (2, D, D)), ("xa_w_kv", (2, D, 2 * D)),
                       ("xa_w_o", (2, D, D)), ("mlp_w_up", (2, D, 4 * D)), ("mlp_w_down", (2, 4 * D, D))]:
            self.w[n] = inp(n, shp)
        self.out = nc.dram_tensor("out", [S_LEN, D], F32, kind="ExternalOutput").ap()
        self.wsc = nc.dram_tensor("wsc", [NBLK, 128, 4096], BF16, kind="Internal").ap()
        self.wsc_b = [Buf("wsc%d" % i) for i in range(NBLK)]

    def layout(self):
        mk = self.mk
        self.ident_f = mk(128, F32, name="identf")
        self.ident_b = mk(64, BF16, name="identb")
        self.ones_b = mk(64, BF16, name="ones")
        self.ones128_b = mk(64, BF16, name="ones128")
        self.maskU = mk(128, name="maskU")
        self.maskL = mk(128, name="maskL")
        self.nm_bd_su = mk(128, name="nm1")
        self.nm_off_u = mk(128, name="nm2")
        self.nm_bd_sl = mk(128, name="nm3")
        self.nm_off_l = mk(128, name="nm4")
        self.sel = mk(512, BF16, "p (h m) -> p h m", name="sel", h=8)
        self.self32 = None
        self.resetm = mk(512, name="resetm")
        self.vecs = mk(NVEC, name="vecs")
        self.gfin = mk(1024, name="gfin")
        self.small = mk(16, name="small")
        self.wba = mk(64, BF16, "p (k n) -> p k n", name="wba", k=8)
        self.KT = [mk(1024, BF16, "p (c m) -> p c m", name="KT%d" % l, c=8) for l in range(2)]
        self.Vm = [mk(1024, BF16, "p (c n) -> p c n", name="V%d" % l, c=2) for l in range(2)]
        self.hT = mk(4096, F32, "p (c t) -> p c t", nbuf=8, name="hT", c=8)
        self.xn = mk(2048, BF16, "p (c t) -> p c t", nbuf=8, name="xn", c=8)
        self.NSLOT = 3
        self.wslot = [mk(2048, BF16, name="wslot%d" % i) for i in range(self.NSLOT)]
        self.S = mk(1024, F32, "p (h d) -> p h d", nbuf=8, name="S", h=8)
        self.Sb = mk(512, BF16, "p (h d) -> p h d", nbuf=8, name="Sb", h=8)
        self.halo = mk(72, F32, "p (c j) -> p c j", nbuf=24, name="halo", c=24)
        self.ubuf = mk(8 * 544 // 2, BF16, "p (c t) -> p c t", nbuf=8, name="ubuf", c=8)
        self.xin = [mk(1024, name="xin%d" % i) for i in range(2)]
        self.ost = self.xin
        self.nsq = [mk(256, BF16, name="nsq%d" % i) for i in range(2)]
        self.nln = mk(512, name="nln")
        self.nrs = mk(512, name="nrs")
        self.ncol = mk(8, name="ncol")
        self.shared0 = self.off
        self.shared_events = {}
        self.extra_bufs = []
        self.pre_stat = None
        self.jit_src = {}

    def phase(self, prev_bufs):
        flat = []
        for b in list(prev_bufs) + self.extra_bufs:
            flat += b.b if isinstance(b, TT) else [b]
        ev = self.P.collect(flat)
        for k, v in ev.items():
            if self.shared_events.get(k, 0) < v:
                self.shared_events[k] = v
        self.off = self.shared0
        return dict(self.shared_events)

    def chk(self, lvl):
        if DN_STOP == lvl:
            self.P.mute = True

    def vcol(self, name, j=0):
        c = VC[name] + j
        return self.vecs.ap[:, c:c + 1]

    def prologue(self):
        P = self.P
        V = self.vecs
        P.dma(lambda e: e.dma_start(out=V.ap, in_=self.vecs_d), "vecs", (), V.b)
        P.dma(lambda e: e.dma_start(out=self.gfin.ap, in_=self.fing_d[0:1, :].partition_broadcast(128)),
              "gfin", (), self.gfin.b)
        g = "gpsimd"
        idf = self.ident_f
        self.memset(g, idf.ap, 0.0, idf.b)
        self.aselect(idf.ap, [[-1, 128]], 1, ALU.not_equal, 1.0, idf.b)
        self.cp(g, self.ident_b.ap, idf.ap, idf.b, self.ident_b.b)
        self.memset(g, self.ones_b.ap, 1.0, self.ones_b.b)
        self.memset(g, self.ones128_b.ap, 128.0, self.ones128_b.b)
        mU, mL = self.maskU, self.maskL
        self.memset(g, mU.ap, 0.0, mU.b)
        self.aselect(mU.ap, [[1, 128]], -1, ALU.is_gt, NEG, mU.b)
        self.memset(g, mL.ap, 0.0, mL.b)
        self.aselect(mL.ap, [[-1, 128]], 1, ALU.is_gt, -NEG, mL.b)
        for m, kind in [(self.nm_bd_su, "bdu"), (self.nm_off_u, "offu"), (self.nm_bd_sl, "bdl"),
                        (self.nm_off_l, "offl")]:
            self.memset(g, m.ap, 0.0, m.b)
            if kind in ("bdu", "bdl"):
                self.memset(g, m.ap[0:64, 0:64], -1.0, m.b)
                self.memset(g, m.ap[64:128, 64:128], -1.0, m.b)
                if kind == "bdu":
                    self.aselect(m.ap, [[1, 128]], -1, ALU.is_gt, 0.0, m.b)
                else:
                    self.aselect(m.ap, [[-1, 128]], 1, ALU.is_gt, 0.0, m.b)
            elif kind == "offu":
                self.memset(g, m.ap[0:64, 64:128], -1.0, m.b)
            else:
                self.memset(g, m.ap[64:128, 0:64], -1.0, m.b)
        sel = self.sel
        self.memset(g, sel.ap[0:8], 0.0, sel.b)
        selflat = sel.ap[0:8].rearrange("p h m -> p (h m)")
        self.P.op(g, lambda e: e.affine_select(out=selflat, in_=selflat, pattern=[[-1, 8], [0, 128]],
                                               compare_op=ALU.not_equal, fill=1.0, base=0,
                                               channel_multiplier=1), sel.b, sel.b)
        rm = self.resetm
        self.memset(g, rm.ap[0:8], 1.0, rm.b)
        for n in range(4):
            self.memset(g, rm.ap[0:8, n * 128:n * 128 + 1], 0.0, rm.b)
        sm = self.small
        self.act(sm.ap[0:8, 0:1], V.ap[0:8, VC["a_log"]:VC["a_log"] + 1], AF.Exp, V.b, sm.b)
        self.ts("vector", sm.ap[0:8, 0:1], sm.ap[0:8, 0:1], -1.0, ALU.mult, sm.b, sm.b)
        for t_ in (self.S, self.Sb, self.halo, self.ubuf):
            self.memset(g, t_.ap, 0.0, t_.b)
        self.prologue_weights()

    def prologue_weights(self):
        P = self.P
        init = self.phase([])
        NR = 3
        st32 = [self.mk(4096, F32, name="st32_%d" % i, init=init) for i in range(NR)]
        st16 = [self.mk(2048, BF16, name="st16_%d" % i, init=init) for i in range(NR)]
        memtok = self.mk(2048, F32, "p (c f) -> p c f", name="memtok", init=init, c=2)
        memg = self.mk(1024, F32, name="memg", init=init)
        memn = self.mk(1024, BF16, "p (c f) -> p c f", name="memn", init=init, c=2)
        memnT = self.mk(1024, BF16, "p (c m) -> p c m", name="memnT", init=init, c=8)
        junk = self.mk(1024, F32, name="junk", init=init)
        self.pro_bufs = st32 + st16 + [memtok, memg, memn, memnT, junk]
        self.cast_i = 0

        def src_kn(wap, col0, ncols, kc):
            return wap.rearrange("(k p) n -> p k n", p=128)[:, :, col0:col0 + ncols]

        def cast_block(src_ap, kc, ncols):
            i = self.cast_i
            self.cast_i += 1
            a32, a16 = st32[i % NR], st16[i % NR]
            n = kc * ncols
            dst = a32.ap[:, 0:n].rearrange("p (k n) -> p k n", k=kc)
            P.dma(lambda e: e.dma_start(out=dst, in_=src_ap), "pl%d" % (i % NR), (), a32.b)
            h = n // 2
            self.cp("vector", a16.ap[:, 0:h], a32.ap[:, 0:h], a32.b, a16.b)
            self.cp("scalar", a16.ap[:, h:n], a32.ap[:, h:n], a32.b, a16.b)
            return a16

        def store_block(a16, blk):
            i = self.cast_i - 1
            P.dma(lambda e: e.dma_start(out=self.wsc[blk], in_=a16.ap), "ps%d" % (i % NR), a16.b,
                  [self.wsc_b[blk]])

        W = self.w
        st = self.stages
        if "l0mix" in st:
            a16 = cast_block(src_kn(W["dn_w_in"], 4096, 16, 8), 8, 16)
            self.cp("vector", self.wba.ap, a16.ap[:, 0:128].rearrange("p (k n) -> p k n", k=8), a16.b,
                    self.wba.b)
        for l in range(2):
            if ("l%dxa" % l) not in st:
                continue
            P.dma(lambda e, l=l: e.dma_start(out=memtok.ap, in_=self.mem.rearrange("(c p) f -> p c f", p=128)),
                  "memtok", (), memtok.b)
            P.dma(lambda e, l=l: e.dma_start(out=memg.ap, in_=self.memg_d[l:l + 1, :].partition_broadcast(128)),
                  "memg", (), memg.b)
            nc_ = self.ncol
            for mc in range(2):
                self.act(junk.ap, memtok.ap[:, mc, :], AF.Square, memtok.b, junk.b + nc_.b,
                         accum_out=nc_.ap[:, 0:1])
                self.act(nc_.ap[:, 1:2], nc_.ap[:, 0:1], AF.Ln, nc_.b, nc_.b, bias=RMS_EPS, scale=1.0 / D)
                self.act(nc_.ap[:, 2:3], nc_.ap[:, 1:2], AF.Exp, nc_.b, nc_.b, scale=-0.5)
                self.stt(memn.ap[:, mc, :], memtok.ap[:, mc, :], nc_.ap[:, 2:3], memg.ap, ALU.mult, ALU.mult,
                         memtok.b + nc_.b + memg.b, memn.b)
            for mc in range(2):
                pa, pb = self.bank()
                pv = pa.bitcast(BF16)
                for kc in range(8):
                    self.tr(pv[:, kc * 128:(kc + 1) * 128], memn.ap[:, mc, kc * 128:(kc + 1) * 128], self.ident_b.ap,
                            memn.b + self.ident_b.b, pb)
                self.cp("vector", memnT.ap[:, :, mc * 128:(mc + 1) * 128], pv.rearrange("p (k m) -> p k m", k=8),
                        pb, memnT.b)
            wkv = W["xa_w_kv"][l]
            for blk in range(4):
                a16 = cast_block(src_kn(wkv, blk * 512, 512, 8), 8, 512)
                wv = a16.ap.rearrange("p (k n) -> p k n", k=8)
                if blk < 2:
                    for oc in range(4):
                        pa, pb = self.bank()
                        for kc in range(8):
                            self.mm(pa[:, 0:256], wv[:, kc, oc * 128:(oc + 1) * 128], memnT.ap[:, kc, :],
                                    kc == 0, kc == 7, a16.b + memnT.b, pb)
                        self.cp("vector", self.KT[l].ap[:, blk * 4 + oc, :], pa[:, 0:256], pb, self.KT[l].b)
                else:
                    for mc in range(2):
                        pa, pb = self.bank()
                        for kc in range(8):
                            self.mm(pa, memnT.ap[:, kc, mc * 128:(mc + 1) * 128], wv[:, kc, :],
                                    kc == 0, kc == 7, a16.b + memnT.b, pb)
                        self.cp("vector", self.Vm[l].ap[:, mc, (blk - 2) * 512:(blk - 1) * 512], pa, pb,
                                self.Vm[l].b)
        self.jit_src = {}

        def do(name, wap, kc, ncols):
            b0, nb = WB[name]
            for j in range(nb):
                if JIT and name not in ("w_in", "w_out"):
                    self.jit_src[b0 + j] = ("w", src_kn(wap, j * ncols, ncols, kc), kc, ncols)
                    continue
                a16 = cast_block(src_kn(wap, j * ncols, ncols, kc), kc, ncols)
                store_block(a16, b0 + j)

        if "l0mix" in st:
            do("w_in", W["dn_w_in"], 8, 512)
            do("w_out", W["dn_w_out"], 8, 512)
        for l in range(2):
            if ("l%dxa" % l) in st:
                do("xa%d_q" % l, W["xa_w_q"][l], 8, 512)
                do("xa%d_o" % l, W["xa_w_o"][l], 8, 512)
            if ("l%dmlp" % l) in st:
                do("mlp%d_up" % l, W["mlp_w_up"][l], 8, 512)
                do("mlp%d_dn" % l, W["mlp_w_down"][l], 32, 128)
        if "l1mix" in st:
            do("pw1", W["cv_w_pw1"], 8, 512)
            do("pw2", W["cv_w_pw2"], 8, 512)
            b0, nb = WB["dwc"]
            for c in range(8):
                if JIT:
                    self.jit_src[b0 + c] = ("dwc", c)
                    continue
                i = self.cast_i
                self.cast_i += 1
                a16 = st16[i % NR]
                dv = a16.ap.rearrange("p (j n) -> p j n", j=32)
                for j in range(31):
                    if j % 2:
                        self.act(dv[:, j, :], self.ident_b.ap, AF.Copy, self.ident_b.b + self.vecs.b, a16.b,
                                 scale=self.vcol("cv_w_dw", j * 8 + c))
                    else:
                        self.ts("vector", dv[:, j, :], self.ident_b.ap,
                                self.vcol("cv_w_dw", j * 8 + c), ALU.mult, self.ident_b.b + self.vecs.b, a16.b)
                store_block(a16, b0 + c)

    def wstream_init(self):
        order = []
        st = self.stages
        for name, stg in [("w_in", "l0mix"), ("w_out", "l0mix"), ("xa0_q", "l0xa"), ("xa0_o", "l0xa"),
                          ("mlp0_up", "l0mlp"), ("mlp0_dn", "l0mlp"), ("pw1", "l1mix"), ("dwc", "l1mix"),
                          ("pw2", "l1mix"), ("xa1_q", "l1xa"), ("xa1_o", "l1xa"), ("mlp1_up", "l1mlp"),
                          ("mlp1_dn", "l1mlp")]:
            if stg in st:
                b0, nb = WB[name]
                order += list(range(b0, b0 + nb))
        self.wseq = order * self.ntiles
        self.w_per_tile = len(order)
        self.jit_st = None
        self.jit_pending = None
        self.jit_i = 0
        self.w_issued = 0
        self.w_consumed = 0

    def w_issue(self):
        i = self.w_issued
        if i >= len(self.wseq):
            return
        slot = self.wslot[i % self.NSLOT]
        blk = self.wseq[i]
        if i < self.w_per_tile and blk in self.jit_src:
            self.jit_issue(blk, slot, i % self.NSLOT)
        else:
            self.jit_flush()
            self.P.dma(lambda e: e.dma_start(out=slot.ap, in_=self.wsc[blk]), "w%d" % (i % self.NSLOT),
                       [self.wsc_b[blk]], slot.b)
        self.w_issued += 1

    def jit_flush(self):
        if self.jit_pending is not None:
            blk, slot, si = self.jit_pending
            self.jit_pending = None
            self.P.dma(lambda e: e.dma_start(out=self.wsc[blk], in_=slot.ap), "js%d" % si, slot.b, [self.wsc_b[blk]])

    def jit_issue(self, blk, slot, si):
        P = self.P
        if self.jit_st is None:
            flat = []
            for b in self.phase_bufs:
                flat += b.b if isinstance(b, TT) else [b]
            init = self.P.collect(flat)
            for k, v in self.shared_events.items():
                if init.get(k, 0) < v:
                    init[k] = v
            self.jit_st = [self.mk(4096, F32, name="jit32_%d" % i, init=init, off=self.shared0 + JIT_OFF + 4096 * i)
                           for i in range(2)]
            self.extra_bufs += self.jit_st
        src = self.jit_src[blk]
        j = self.jit_i
        self.jit_i += 1
        if src[0] == "w":
            _, src_ap, kc, ncols = src
            a32 = self.jit_st[j % 2]
            n = kc * ncols
            dst = a32.ap[:, 0:n].rearrange("p (k n) -> p k n", k=kc)
            P.dma(lambda e: e.dma_start(out=dst, in_=src_ap), "jl%d" % (j % 2), (), a32.b)
            self.jit_flush()
            h = n // 2
            self.cp("vector", slot.ap[:, 0:h], a32.ap[:, 0:h], a32.b, slot.b)
            self.cp("scalar", slot.ap[:, h:n], a32.ap[:, h:n], a32.b, slot.b)
        else:
            c = src[1]
            self.jit_flush()
            dv = slot.ap.rearrange("p (j n) -> p j n", j=32)
            for jj in range(31):
                if jj % 2:
                    self.act(dv[:, jj, :], self.ident_b.ap, AF.Copy, self.ident_b.b + self.vecs.b, slot.b,
                             scale=self.vcol("cv_w_dw", jj * 8 + c))
                else:
                    self.ts("vector", dv[:, jj, :], self.ident_b.ap, self.vcol("cv_w_dw", jj * 8 + c), ALU.mult,
                            self.ident_b.b + self.vecs.b, slot.b)
            self.memset("vector", dv[:, 31, :], 0.0, slot.b)
        self.jit_pending = (blk, slot, si)

    def w_acquire(self, blk):
        i = self.w_consumed
        assert self.wseq[i] == blk, (i, self.wseq[i], blk)
        while self.w_issued < min(len(self.wseq), i + self.NSLOT):
            self.w_issue()
        self.w_consumed += 1
        return self.wslot[i % self.NSLOT]

    def rmsnorm_xn(self, gname):
        hT, xn = self.hT, self.xn
        if self.pre_stat is not None:
            pa, pb = self.pre_stat
            self.pre_stat = None
        else:
            pa, pb = self.bank(4, 8)
            for c in range(8):
                sq = self.nsq[c % 2]
                self.act(sq.ap, hT.ap[:, c, :], AF.Square, [hT.b[c]], sq.b)
                self.mm(pa, self.ones_b.ap, sq.ap, c == 0, c == 7, self.ones_b.b + sq.b, pb)
        self.act(self.nln.ap, pa, AF.Ln, pb, self.nln.b, bias=RMS_EPS, scale=1.0 / D)
        self.act(self.nrs.ap, self.nln.ap, AF.Exp, self.nln.b, self.nrs.b, scale=-0.5)
        for c in range(8):
            self.stt(xn.ap[:, c, :], hT.ap[:, c, :], self.vcol(gname, c), self.nrs.ap, ALU.mult, ALU.mult,
                     [hT.b[c]] + self.nrs.b + self.vecs.b, [xn.b[c]])

    def proj(self, xin, name, evac, kc_n=8, ncols=512):
        b0, nb = WB[name]
        n_oc = ncols // 128
        for j in range(nb):
            slot = self.w_acquire(b0 + j)
            wv = slot.ap.rearrange("p (k n) -> p k n", k=kc_n)
            if j == 0 and n_oc == 4:
                banks = [self.bank(0, 4) for _ in range(n_oc)]
                for kc in range(kc_n):
                    for oc in range(n_oc):
                        pa, pb = banks[oc]
                        self.mm(pa, wv[:, kc, oc * 128:(oc + 1) * 128], xin.ap[:, kc, :], kc == 0, kc == kc_n - 1,
                                slot.b + [xin.b[kc]], pb)
                for oc in range(n_oc):
                    evac(oc, banks[oc][0], banks[oc][1])
                continue
            for oc in range(n_oc):
                pa, pb = self.bank(0, 4)
                for kc in range(kc_n):
                    self.mm(pa, wv[:, kc, oc * 128:(oc + 1) * 128], xin.ap[:, kc, :], kc == 0, kc == kc_n - 1,
                            slot.b + [xin.b[kc]], pb)
                evac(j * n_oc + oc, pa, pb)

    def resid_evac(self, bias_name=None, pre=True):
        hT = self.hT
        st_ = {}
        if pre:
            st_["bank"] = self.bank(4, 8)

        def stat(c):
            sa, sb_ = st_["bank"]
            sq = self.nsq[c % 2]
            self.mm(sa, self.ones_b.ap, sq.ap, c == 0, c == 7, self.ones_b.b + sq.b, sb_)
            if c == 7:
                self.pre_stat = (sa, sb_)

        def f(oc, pa, pb):
            if bias_name is None:
                self.tt("vector", hT.ap[:, oc, :], hT.ap[:, oc, :], pa, ALU.add, pb + [hT.b[oc]], [hT.b[oc]])
            else:
                self.stt(hT.ap[:, oc, :], pa, self.vcol(bias_name, oc), hT.ap[:, oc, :], ALU.add, ALU.add,
                         pb + [hT.b[oc]] + self.vecs.b, [hT.b[oc]])
            if pre:
                if oc > 0:
                    stat(oc - 1)
                sq = self.nsq[oc % 2]
                self.act(sq.ap, hT.ap[:, oc, :], AF.Square, [hT.b[oc]], sq.b)
                if oc == 7:
                    stat(7)
        return f

    def load_tile(self, t):
        P = self.P
        hT = self.hT
        self.pre_stat = None
        for sub in range(4):
            xi = self.xin[sub % 2]
            r0 = t * T + sub * 128
            P.dma(lambda e, xi=xi, r0=r0: e.dma_start(out=xi.ap, in_=self.x[r0:r0 + 128, :]),
                  "x%d" % (sub % 2), (), xi.b)
            pa, pb = self.bank2()
            for c in range(8):
                self.tr(pa[:, c * 128:(c + 1) * 128], xi.ap[:, c * 128:(c + 1) * 128], self.ident_f.ap,
                        xi.b + self.ident_f.b, [pb[c // 4]])
            for hf in range(2):
                src = pa[:, hf * 512:(hf + 1) * 512].rearrange("p (c t) -> p c t", c=4)
                dst = hT.ap[:, hf * 4:(hf + 1) * 4, sub * 128:(sub + 1) * 128]
                self.cp("vector" if hf else "scalar", dst, src, [pb[hf]],
                        hT.b[hf * 4:(hf + 1) * 4])

    def store_tile(self, t):
        P = self.P
        hT = self.hT
        for sub in range(4):
            os_ = self.ost[sub % 2]
            r0 = t * T + sub * 128
            pa, pb = self.bank2()
            for c in range(8):
                self.tr(pa[:, c * 128:(c + 1) * 128], hT.ap[:, c, sub * 128:(sub + 1) * 128], self.ident_f.ap,
                        [hT.b[c]] + self.ident_f.b, [pb[c // 4]])
            if self.final_norm:
                nc_ = self.ncol
                self.act(os_.ap, pa, AF.Square, pb, os_.b + nc_.b, accum_out=nc_.ap[:, 4:5])
                self.act(nc_.ap[:, 5:6], nc_.ap[:, 4:5], AF.Ln, nc_.b, nc_.b, bias=RMS_EPS, scale=1.0 / D)
                self.act(nc_.ap[:, 6:7], nc_.ap[:, 5:6], AF.Exp, nc_.b, nc_.b, scale=-0.5)
                self.stt(os_.ap, pa, nc_.ap[:, 6:7], self.gfin.ap, ALU.mult, ALU.mult,
                         pb + nc_.b + self.gfin.b, os_.b)
            else:
                self.cp("vector", os_.ap, pa, pb, os_.b)
            P.dma(lambda e, os_=os_, r0=r0: e.dma_start(out=self.out[r0:r0 + 128, :], in_=os_.ap),
                  "o%d" % (sub % 2), os_.b, ())

    def mlp(self, l):
        init = self.phase(self.phase_bufs)
        mid = self.mk(8192, BF16, "p (c t) -> p c t", nbuf=32, name="mid", init=init, c=32)
        rt = [self.mk(512, F32, name="rt%d" % i, init=init) for i in range(2)]
        self.phase_bufs = [mid] + rt
        self.rmsnorm_xn("mlp_norm%d" % l)

        def up_evac(oc, pa, pb):
            r = rt[oc % 2]
            self.act(r.ap, pa, AF.Relu, pb, r.b)
            self.tt("vector" if oc % 2 else "gpsimd", mid.ap[:, oc, :], r.ap, r.ap, ALU.mult, r.b, [mid.b[oc]])

        self.proj(self.xn, "mlp%d_up" % l, up_evac)
        self.proj(mid, "mlp%d_dn" % l, self.resid_evac(pre=(l == 0 and "l1mix" in self.stages)), kc_n=32, ncols=128)

    def xattn(self, l):
        init = self.phase(self.phase_bufs)
        qT = self.mk(2048, BF16, "p (c t) -> p c t", nbuf=8, name="xq", init=init, c=8)
        oxT = self.mk(2048, BF16, "p (c t) -> p c t", nbuf=8, name="xo", init=init, c=8)
        pT = [self.mk(512, BF16, "p (c t) -> p c t", nbuf=2, name="pT%d" % i, init=init, c=2) for i in range(2)]
        lnd = self.mk(512, F32, name="lnd", init=init)
        rden = [self.mk(512, F32, name="rden%d" % i, init=init) for i in range(2)]
        self.phase_bufs = [qT, oxT, lnd] + pT + rden
        self.rmsnorm_xn("xa_norm%d" % l)

        def q_evac(oc, pa, pb):
            self.cp("scalar" if oc % 2 else "vector", qT.ap[:, oc, :], pa, pb, [qT.b[oc]])

        self.proj(self.xn, "xa%d_q" % l, q_evac)
        KT, Vm = self.KT[l], self.Vm[l]
        def scores(a):
            pss = []
            for mc in range(2):
                pa, pb = self.bank()
                for dc in range(2):
                    self.mm(pa, KT.ap[:, 2 * a + dc, mc * 128:(mc + 1) * 128], qT.ap[:, 2 * a + dc, :],
                            dc == 0, dc == 1, KT.b + [qT.b[2 * a + dc]], pb)
                pss.append((pa, pb))
            return pss

        def rest(a, pss):
            p_ = pT[a % 2]
            rd = rden[a % 2]
            for mc in range(2):
                pa, pb = pss[mc]
                self.act(p_.ap[:, mc, :], pa, AF.Exp, pb, [p_.b[mc]], scale=1.0 / 16.0)
            pd, pdb = self.bank()
            for mc in range(2):
                self.mm(pd, self.ones_b.ap, p_.ap[:, mc, :], mc == 0, mc == 1, self.ones_b.b + [p_.b[mc]], pdb)
            pos = []
            for dvc in range(2):
                pa, pb = self.bank()
                for mc in range(2):
                    c0 = a * 256 + dvc * 128
                    self.mm(pa, Vm.ap[:, mc, c0:c0 + 128], p_.ap[:, mc, :], mc == 0, mc == 1,
                            Vm.b + [p_.b[mc]], pb)
                pos.append((pa, pb))
            self.act(lnd.ap, pd, AF.Ln, pdb, lnd.b)
            self.act(rd.ap, lnd.ap, AF.Exp, lnd.b, rd.b, scale=-1.0)
            for dvc in range(2):
                pa, pb = pos[dvc]
                self.tt("vector", oxT.ap[:, 2 * a + dvc, :], pa, rd.ap, ALU.mult, pb + rd.b, [oxT.b[2 * a + dvc]])

        prev = None
        for a in range(4):
            cur = scores(a)
            if prev is not None:
                rest(a - 1, prev)
            prev = cur
        rest(3, prev)
        self.proj(oxT, "xa%d_o" % l, self.resid_evac())

    def conformer(self):
        init = self.phase(self.phase_bufs)
        A = self.mk(4096, F32, "p (c t) -> p c t", nbuf=8, name="cvA", init=init, c=8)
        cf = self.mk(4096, F32, "p (c t) -> p c t", nbuf=8, name="cvC", init=init, c=8)
        sg = [self.mk(512, F32, name="cvsg%d" % i, init=init) for i in range(2)]
        cb = [self.mk(256, BF16, name="cvcb%d" % i, init=init) for i in range(2)]
        cq = [self.mk(256, BF16, name="cvcq%d" % i, init=init) for i in range(2)]
        mean = self.mk(512, F32, name="cvmean", init=init)
        var = self.mk(512, F32, name="cvvar", init=init)
        rstd = self.mk(512, F32, name="cvrstd", init=init)
        t1 = [self.mk(512, F32, name="cvt%d" % i, init=init) for i in range(2)]
        self.phase_bufs = [A, cf, mean, var, rstd] + sg + cb + cq + t1
        ub = self.ubuf
        self.rmsnorm_xn("cv_norm")

        def pw1_evac(oc, pa, pb):
            if oc < 8:
                self.act(A.ap[:, oc, :], pa, AF.Identity, pb + self.vecs.b, [A.b[oc]],
                         bias=self.vcol("cv_b_pw1", oc))
            else:
                c = oc - 8
                s_ = sg[c % 2]
                self.act(s_.ap, pa, AF.Sigmoid, pb + self.vecs.b, s_.b, bias=self.vcol("cv_b_pw1", oc))
                self.tt("vector" if c % 2 else "gpsimd", ub.ap[:, c, 30:542], A.ap[:, c, :], s_.ap, ALU.mult,
                        [A.b[c]] + s_.b, [ub.b[c]])

        self.proj(self.xn, "pw1", pw1_evac)
        b0, nb = WB["dwc"]
        ps1, ps1b = self.bank(4, 8)
        ps2, ps2b = self.bank(4, 8)

        def cstat(c):
            b_, q_ = cb[c % 2], cq[c % 2]
            self.mm(ps1, self.ones_b.ap, b_.ap, c == 0, c == 7, self.ones_b.b + b_.b, ps1b)
            self.mm(ps2, self.ones_b.ap, q_.ap, c == 0, c == 7, self.ones_b.b + q_.b, ps2b)

        for c in range(8):
            slot = self.w_acquire(b0 + c)
            dv = slot.ap.rearrange("p (j n) -> p j n", j=32)
            pa, pb = self.bank(0, 4)
            for j in range(31):
                self.mm(pa, dv[:, j, :], ub.ap[:, c, j:j + 512], j == 0, j == 30, slot.b + [ub.b[c]], pb)
            self.act(cf.ap[:, c, :], pa, AF.Identity, pb + self.vecs.b, [cf.b[c]], bias=self.vcol("cv_b_dw", c))
            self.cp("gpsimd", ub.ap[:, c, 0:30], ub.ap[:, c, 512:542], [ub.b[c]], [ub.b[c]])
            b_, q_ = cb[c % 2], cq[c % 2]
            self.cp("gpsimd", b_.ap, cf.ap[:, c, :], [cf.b[c]], b_.b)
            self.act(q_.ap, cf.ap[:, c, :], AF.Square, [cf.b[c]], q_.b)
            if c > 0:
                cstat(c - 1)
        cstat(7)
        self.ts("vector", mean.ap, ps1, 1.0 / D, ALU.mult, ps1b, mean.b)
        self.tt("gpsimd", var.ap, mean.ap, mean.ap, ALU.mult, mean.b, var.b)
        self.stt(var.ap, ps2, 1.0 / D, var.ap, ALU.mult, ALU.subtract, ps2b + var.b, var.b)
        self.act(var.ap, var.ap, AF.Ln, var.b, var.b, bias=LN_EPS)
        self.act(rstd.ap, var.ap, AF.Exp, var.b, rstd.b, scale=-0.5)
        xn = self.xn
        for c in range(8):
            t_ = t1[c % 2]
            self.tt("vector", t_.ap, cf.ap[:, c, :], mean.ap, ALU.subtract, [cf.b[c]] + mean.b, t_.b)
            self.tt("gpsimd" if c % 2 else "vector", t_.ap, t_.ap, rstd.ap, ALU.mult, t_.b + rstd.b, t_.b)
            self.act(xn.ap[:, c, :], t_.ap, AF.Silu, t_.b + self.vecs.b, [xn.b[c]],
                     bias=self.vcol("cv_ln_b", c), scale=self.vcol("cv_ln_g", c))
        self.proj(xn, "pw2", self.resid_evac("cv_b_pw2"))

    def deltanet(self):
        P = self.P
        init = self.phase(self.phase_bufs)
        mk = self.mk
        qT = mk(2048, BF16, "p (c t) -> p c t", nbuf=8, name="dq", init=init, c=8)
        kT = mk(2048, BF16, "p (c t) -> p c t", nbuf=8, name="dk", init=init, c=8)
        vT = mk(2048, BF16, "p (c t) -> p c t", nbuf=8, name="dv", init=init, c=8)
        zs = mk(2048, BF16, "p (c t) -> p c t", nbuf=8, name="dz", init=init, c=8)
        rows = mk(2048, F32, "p (r t) -> p r t", nbuf=4, name="rows", init=init, r=4)
        rtmp = mk(512, F32, name="rtmp", init=init)
        rtmp2 = mk(512, F32, name="rtmp2", init=init)
        rsp = mk(1280, BF16, "p (r t) -> p r t", name="rsp", init=init, r=5)
        kbT = mk(512, BF16, "p (h t) -> p h t", nbuf=8, name="kbT", init=init, h=8)
        kbgT = mk(512, BF16, "p (h t) -> p h t", nbuf=8, name="kbgT", init=init, h=8)
        qdT = mk(512, BF16, "p (h t) -> p h t", nbuf=8, name="qdT", init=init, h=8)
        oTc = mk(1024, F32, "p (h t) -> p h t", nbuf=8, name="oTc", init=init, h=8)
        osq = [mk(256, BF16, name="osq%d" % i, init=init) for i in range(2)]
        ors = mk(1024, F32, name="ors", init=init)
        ogt = oTc
        cols = mk(48, F32, name="cols", init=init)
        G = 4
        off_alias = self.off
        ctmp = [mk(516, F32, name="ctmp%d" % i, init=init) for i in range(3)]
        cacc = [mk(512, F32, name="cacc%d" % i, init=init) for i in range(3)]
        cqs = [mk(512, F32, name="cqs%d" % i, init=init) for i in range(4)]
        cth = [mk(512, F32, name="cth%d" % i, init=init) for i in range(2)]
        csq = [mk(256, BF16, name="csq%d" % i, init=init) for i in range(2)]
        cln = [mk(512, F32, name="cln%d" % i, init=init) for i in range(2)]
        end_rings = self.off
        self.off = off_alias

        def g4(words, dt, name):
            return mk(words * G, dt, "p (h t) -> p h t", nbuf=G, name=name, init=init, h=G)

        gcb = g4(128, F32, "gcb")
        E1 = g4(128, F32, "E1")
        bE = g4(128, F32, "bE")
        Eu = g4(128, F32, "Eu")
        El = g4(128, F32, "El")
        Nt = g4(128, F32, "Nt")
        attnT = g4(64, BF16, "attnT")
        Nbd = [g4(64, BF16, "Nbd%d" % i) for i in range(2)]
        Mbd = [g4(64, BF16, "Mbd%d" % i) for i in range(2)]
        Noff = g4(64, BF16, "Noff")
        Moff = g4(64, BF16, "Moff")
        Pm = g4(64, BF16, "Pm")
        Qm = g4(64, BF16, "Qm")
        Ym = g4(64, BF16, "Ym")
        TTm = g4(64, BF16, "TTm")
        kdec = g4(64, BF16, "kdec")
        vb = g4(128, F32, "vb")
        rr = g4(64, BF16, "rr")
        vnew = g4(64, BF16, "vnew")
        g4_all = [gcb, E1, bE, Eu, El, Nt, attnT, Noff, Moff, Pm, Qm, Ym, TTm, kdec, vb, rr, vnew] + Nbd + Mbd
        self.off = max(self.off, end_rings)
        self.phase_bufs = ([qT, kT, vT, zs, rows, rtmp, rtmp2, rsp, kbT, kbgT, qdT, oTc, ors, ogt, cols, gcb, E1, bE, Eu,
                            El, Nt, attnT, Noff, Moff, Pm, Qm, Ym, TTm, kdec, vb, rr, vnew]
                           + ctmp + cacc + cqs + csq + cln + cth + Nbd + Mbd + osq)
        xn, S, Sb, halo = self.xn, self.S, self.Sb, self.halo
        vecs_b = self.vecs.b

        self.rmsnorm_xn("dn_norm")

        wc = lambda oc, j: self.vcol("dn_w_conv", j * 24 + oc)
        statb = {}

        def stA(oc, pa, pb):
            tm, ac = ctmp[oc % 3], cacc[oc % 3]
            self.cp("gpsimd", tm.ap[:, 0:3], halo.ap[:, oc, :], [halo.b[oc]], tm.b)
            self.act(tm.ap[:, 3:515], pa, AF.Copy, pb, tm.b)
            self.act(ac.ap, pa, AF.Copy, pb + vecs_b, ac.b, scale=wc(oc, 3))
            self.cp("gpsimd", halo.ap[:, oc, :], tm.ap[:, 512:515], tm.b, [halo.b[oc]])
            for j in (2, 1, 0):
                self.stt(ac.ap, tm.ap[:, j:j + 512], wc(oc, j), ac.ap, ALU.mult, ALU.add, tm.b + ac.b + vecs_b, ac.b)

        def stB(oc):
            ac = cacc[oc % 3]
            if oc >= 16:
                c = oc - 16
                self.act(vT.ap[:, c, :], ac.ap, AF.Silu, ac.b, [vT.b[c]])
                return
            qs, sq = cqs[oc % 4], csq[oc % 2]
            self.act(qs.ap, ac.ap, AF.Silu, ac.b, qs.b)
            self.tt("gpsimd", sq.ap, qs.ap, qs.ap, ALU.mult, qs.b, sq.b)

        def stB2(oc):
            if not (0 <= oc < 16):
                return
            sq = csq[oc % 2]
            bi = 4 + oc % 4
            sa, sb_ = self.ps[:, bi * 512:(bi + 1) * 512], [self.pbk[bi]]
            ones = self.ones128_b if oc < 8 else self.ones_b
            self.mm(sa, ones.ap, sq.ap, True, True, ones.b + sq.b, sb_)
            statb[oc] = (sa, sb_)

        def stC(ocs):
            ocs = [oc for oc in ocs if 0 <= oc < 16]
            for oc in ocs:
                ln_ = cln[oc % 2]
                sa, sb_ = statb[oc]
                self.act(ln_.ap, sa, AF.Ln, sb_, ln_.b, bias=(128e-6 if oc < 8 else 1e-6))
            for oc in ocs:
                ln_ = cln[oc % 2]
                self.act(ln_.ap, ln_.ap, AF.Exp, ln_.b, ln_.b, scale=-0.5)
            for oc in ocs:
                qs, ln_ = cqs[oc % 4], cln[oc % 2]
                statb.pop(oc)
                isq = oc < 8
                dst = qT if isq else kT
                c = oc if isq else oc - 8
                self.tt("vector", dst.ap[:, c, :], qs.ap, ln_.ap, ALU.mult, qs.b + ln_.b, [dst.b[c]])

        def in_evac(oc, pa, pb):
            if oc >= 24:
                c = oc - 24
                self.act(zs.ap[:, c, :], pa, AF.Silu, pb, [zs.b[c]])
            else:
                stA(oc, pa, pb)
            if 0 <= oc - 1 < 24:
                stB(oc - 1)
            stB2(oc - 2)
            if (oc - 3) % 2 == 1:
                stC([oc - 4, oc - 3])

        self.proj(xn, "w_in", in_evac)
        ring_ev = self.P.collect([b for t_ in (ctmp + cacc + cqs + csq + cln + cth) for b in t_.b])
        for t_ in g4_all:
            for b in t_.b:
                for k_, v_ in ring_ev.items():
                    if b.r.get(k_, 0) < v_:
                        b.r[k_] = v_
        for z_ in (Nbd[0], Mbd[0], Noff, Moff):
            self.memset("gpsimd", z_.ap, 0.0, z_.b)

        self.chk(1)
        pB, pBb = self.bank(4, 8)
        pA, pAb = self.bank(4, 8)
        for kc in range(8):
            self.mm(pB[0:8, :], self.wba.ap[:, kc, 0:8], xn.ap[:, kc, :], kc == 0, kc == 7, self.wba.b + [xn.b[kc]], pBb)
        for kc in range(8):
            self.mm(pA[0:8, :], self.wba.ap[:, kc, 8:16], xn.ap[:, kc, :], kc == 0, kc == 7, self.wba.b + [xn.b[kc]], pAb)
        R = rows.ap
        beta_r, g_r, gc_r, gl_r = R[0:8, 0, :], R[0:8, 1, :], R[0:8, 2, :], R[0:8, 3, :]
        rt = rtmp.ap[0:8, :]
        self.chk(11)
        self.act(beta_r, pB[0:8, :], AF.Sigmoid, pBb, [rows.b[0]])
        self.chk(12)
        self.act(rt, pA[0:8, :], AF.Exp, pAb + vecs_b, rtmp.b, bias=self.vecs.ap[0:8, VC["dt_bias"]:VC["dt_bias"] + 1])
        self.act(rt, rt, AF.Ln, rtmp.b, rtmp.b, bias=1.0)
        self.chk(13)
        self.ts("vector", g_r, rt, self.small.ap[0:8, 0:1], ALU.mult, rtmp.b + self.small.b, [rows.b[1]])
        self.chk(14)
        P.op("vector", lambda e: e.tensor_tensor_scan(out=gc_r, data0=self.resetm.ap[0:8, :], data1=g_r, initial=0.0,
                                                      op0=ALU.mult, op1=ALU.add),
             self.resetm.b + [rows.b[1]], [rows.b[2]])
        self.chk(15)
        for n in range(4):
            self.ts("vector", gl_r[:, n * 128:(n + 1) * 128], gc_r[:, n * 128:(n + 1) * 128], 0.0, ALU.mult,
                    [rows.b[2]], [rows.b[3]], s2=gc_r[:, n * 128 + 127:n * 128 + 128], op1=ALU.add)

        SP_ = rsp.ap
        gch, gcm, gcl, bhi, blo = (SP_[0:8, 0, :], SP_[0:8, 1, :], SP_[0:8, 2, :], SP_[0:8, 3, :], SP_[0:8, 4, :])
        rt2 = rtmp2.ap[0:8, :]
        self.cp("vector", gch, gc_r, [rows.b[2]], rsp.b)
        self.tt("vector", rt, gc_r, gch, ALU.subtract, [rows.b[2]] + rsp.b, rtmp.b)
        self.cp("vector", gcm, rt, rtmp.b, rsp.b)
        self.tt("vector", rt2, rt, gcm, ALU.subtract, rtmp.b + rsp.b, rtmp2.b)
        self.cp("vector", gcl, rt2, rtmp2.b, rsp.b)
        self.cp("vector", bhi, beta_r, [rows.b[0]], rsp.b)
        self.tt("vector", rt2, beta_r, bhi, ALU.subtract, [rows.b[0]] + rsp.b, rtmp2.b)
        self.cp("vector", blo, rt2, rtmp2.b, rsp.b)
        self.chk(2)
        cl = cols.ap
        gc_c, be_c, gl_c, kd_c, eg_c, tp_c = (cl[:, 0:8], cl[:, 8:16], cl[:, 16:24], cl[:, 24:32], cl[:, 32:40],
                                              cl[:, 40:48])
        idb, idf = self.ident_b, self.ident_f
        for n in range(4):
            cs = slice(n * 128, (n + 1) * 128)
            pc, pcb = self.bank()
            for i, rsrc in enumerate((gc_r, beta_r, gl_r)):
                self.tr(pc[:, i * 8:(i + 1) * 8], rsrc[:, cs], idf.ap[0:8, 0:8], [rows.b[(2, 0, 3)[i]]] + idf.b, pcb)
            self.chk(21)
            self.cp("vector", cl[:, 0:24], pc[:, 0:24], pcb, cols.b)
            self.chk(22)
            self.tt("vector", tp_c, gl_c, gc_c, ALU.subtract, cols.b, cols.b)
            self.chk(23)
            self.act(kd_c, tp_c, AF.Exp, cols.b, cols.b)
            self.chk(24)
            self.act(eg_c, gl_c, AF.Exp, cols.b, cols.b)
            self.chk(3)

            def pipe(g0, hs):
                h0 = hs[0]
                i0_ = h0 - g0
                I2 = slice(i0_, i0_ + 2)
                H2 = slice(h0, h0 + 2)
                sl = lambda h: h - g0

                class GB:
                    def __init__(gself):
                        gself.pa, gself.pb = self.bank()
                        gself.pair = gself.pa[:, 0:256].rearrange("p (h t) -> p h t", h=2)
                        bfv = gself.pa.bitcast(BF16)[:, 0:512].rearrange("p (h t) -> p h t", h=2)
                        gself.pair_bf = bfv[:, :, 0:128]

                    def q(gself, h):
                        j = h - h0
                        return gself.pa[:, j * 128:(j + 1) * 128]

                    def qbf(gself, h):
                        j = h - h0
                        return gself.pa[:, j * 128:(j + 1) * 128].bitcast(BF16)[:, 0:128]

                def bl(t_, idx):
                    return [t_.b[k] for k in range(idx.start, idx.stop)]

                pg, pbt = GB(), GB()
                for h in hs:
                    for j_, part in enumerate((gch, gcm, gcl)):
                        self.mm(pg.q(h), self.sel.ap[0:8, h, :], part[:, cs], j_ == 0, j_ == 2, self.sel.b + rsp.b, pg.pb)
                    for j_, part in enumerate((bhi, blo)):
                        self.mm(pbt.q(h), self.sel.ap[0:8, h, :], part[:, cs], j_ == 0, j_ == 1, self.sel.b + rsp.b, pbt.pb)
                yield
                self.act(E1.ap[:, I2, :], pg.pair, AF.Exp, pg.pb, bl(E1, I2))
                self.cp("scalar", gcb.ap[:, I2, :], pg.pair, pg.pb, bl(gcb, I2))
                self.tt("vector", kbT.ap[:, H2, :], kT.ap[:, H2, cs], pbt.pair, ALU.mult, bl(kT, H2) + pbt.pb, bl(kbT, H2))
                self.tt("vector", bE.ap[:, I2, :], E1.ap[:, I2, :], pbt.pair, ALU.mult, bl(E1, I2) + pbt.pb, bl(bE, I2))
                self.tt("gpsimd", kbgT.ap[:, H2, :], kT.ap[:, H2, cs], bE.ap[:, I2, :], ALU.mult, bl(kT, H2) + bl(bE, I2), bl(kbgT, H2))
                self.tt("gpsimd", qdT.ap[:, H2, :], qT.ap[:, H2, cs], E1.ap[:, I2, :], ALU.mult, bl(qT, H2) + bl(E1, I2), bl(qdT, H2))
                for h in hs:
                    i = sl(h)
                    self.stt(Eu.ap[:, i, :], gcb.ap[:, i, :], gc_c[:, h:h + 1], self.maskU.ap, ALU.subtract, ALU.add,
                             [gcb.b[i]] + cols.b + self.maskU.b, [Eu.b[i]])
                    self.stt(El.ap[:, i, :], gcb.ap[:, i, :], gc_c[:, h:h + 1], self.maskL.ap, ALU.subtract, ALU.add,
                             [gcb.b[i]] + cols.b + self.maskL.b, [El.b[i]])
                self.act(Eu.ap[:, I2, :], Eu.ap[:, I2, :], AF.Exp, bl(Eu, I2), bl(Eu, I2))
                self.act(El.ap[:, I2, :], El.ap[:, I2, :], AF.Exp, bl(El, I2), bl(El, I2), scale=-1.0)
                yield
                pkkb, pkbk, pqk = GB(), GB(), GB()
                pkv_a, pkv_b = self.bank()
                for h in hs:
                    self.mm(pkkb.q(h), kT.ap[:, h, cs], kbT.ap[:, h, :], True, True, [kT.b[h], kbT.b[h]], pkkb.pb)
                    self.mm(pkbk.q(h), kbT.ap[:, h, :], kT.ap[:, h, cs], True, True, [kT.b[h], kbT.b[h]], pkbk.pb)
                    self.mm(pqk.q(h), kT.ap[:, h, cs], qT.ap[:, h, cs], True, True, [kT.b[h], qT.b[h]], pqk.pb)
                pkvb = pkv_a.bitcast(BF16)
                for h in hs:
                    j = h - h0
                    self.tr(pkvb[:, (2 * j) * 128:(2 * j + 1) * 128], kT.ap[:, h, cs], idb.ap, [kT.b[h]] + idb.b, pkv_b)
                    self.tr(pkvb[:, (2 * j + 1) * 128:(2 * j + 2) * 128], vT.ap[:, h, cs], idb.ap, [vT.b[h]] + idb.b, pkv_b)
                yield
                for h in hs:
                    i = sl(h)
                    self.tt("gpsimd", Nt.ap[:, i, :], Eu.ap[:, i, :], idf.ap, ALU.add, [Eu.b[i]] + idf.b, [Nt.b[i]])
                self.tt("vector", attnT.ap[:, I2, :], pqk.pair, Nt.ap[:, I2, :], ALU.mult, pqk.pb + bl(Nt, I2), bl(attnT, I2))
                for (r0, c0) in ((0, 0), (64, 64)):
                    rs_, cs_ = slice(r0, r0 + 64), slice(c0, c0 + 64)
                    self.stt(Nbd[0].ap[rs_, I2, cs_], pkkb.pair[rs_, :, cs_], -1.0, Eu.ap[rs_, I2, cs_], ALU.mult, ALU.mult,
                             pkkb.pb + bl(Eu, I2), bl(Nbd[0], I2))
                    self.stt(Mbd[0].ap[rs_, I2, cs_], pkbk.pair[rs_, :, cs_], -1.0, El.ap[rs_, I2, cs_], ALU.mult, ALU.mult,
                             pkbk.pb + bl(El, I2), bl(Mbd[0], I2))
                self.stt(Noff.ap[0:64, I2, 64:128], pkkb.pair[0:64, :, 64:128], -1.0, Eu.ap[0:64, I2, 64:128], ALU.mult, ALU.mult,
                         pkkb.pb + bl(Eu, I2), bl(Noff, I2))
                self.stt(Moff.ap[64:128, I2, 0:64], pkbk.pair[64:128, :, 0:64], -1.0, El.ap[64:128, I2, 0:64], ALU.mult, ALU.mult,
                         pkbk.pb + bl(El, I2), bl(Moff, I2))
                for h in hs:
                    i = sl(h)
                    j = h - h0
                    self.tt("gpsimd", Pm.ap[:, i, :], Nbd[0].ap[:, i, :], idb.ap, ALU.add, [Nbd[0].b[i]] + idb.b, [Pm.b[i]])
                    self.ts("vector", kdec.ap[:, i, :], pkvb[:, (2 * j) * 128:(2 * j + 1) * 128], kd_c[:, h:h + 1], ALU.mult,
                            pkv_b + cols.b, [kdec.b[i]])
                    self.ts("vector", vb.ap[:, i, :], pkvb[:, (2 * j + 1) * 128:(2 * j + 2) * 128], be_c[:, h:h + 1], ALU.mult,
                            pkv_b + cols.b, [vb.b[i]])
                yield
                cur = 0
                for k in range(1, 7):
                    nx = 1 - cur
                    pn = GB() if k < 5 else None
                    pm = GB() if k < 6 else None
                    pp = GB() if k > 1 else None
                    for h in hs:
                        i = sl(h)
                        if k > 1:
                            self.mm(pp.q(h), Mbd[cur].ap[:, i, :], Pm.ap[:, i, :], True, True, [Mbd[cur].b[i], Pm.b[i]], pp.pb)
                        if k < 5:
                            self.mm(pn.q(h), Mbd[cur].ap[:, i, :], Nbd[cur].ap[:, i, :], True, True,
                                    [Mbd[cur].b[i], Nbd[cur].b[i]], pn.pb)
                        if k < 6:
                            self.mm(pm.q(h), Nbd[cur].ap[:, i, :], Mbd[cur].ap[:, i, :], True, True,
                                    [Mbd[cur].b[i], Nbd[cur].b[i]], pm.pb)
                    yield
                    if k > 1:
                        self.tt("vector", Pm.ap[:, I2, :], Pm.ap[:, I2, :], pp.pair, ALU.add, bl(Pm, I2) + pp.pb, bl(Pm, I2))
                    if k < 5:
                        self.cp("scalar", Nbd[nx].ap[:, I2, :], pn.pair, pn.pb, bl(Nbd[nx], I2))
                    if k < 6:
                        self.cp("vector", Mbd[nx].ap[:, I2, :], pm.pair, pm.pb, bl(Mbd[nx], I2))
                    yield
                    cur = nx
                pqs, pys = GB(), GB()
                for h in hs:
                    i = sl(h)
                    self.tr(pqs.qbf(h), Pm.ap[:, i, :], idb.ap, [Pm.b[i]] + idb.b, pqs.pb)
                    self.mm(pys.q(h), Moff.ap[:, i, :], Pm.ap[:, i, :], True, True, [Moff.b[i], Pm.b[i]], pys.pb)
                yield
                self.cp("scalar", Qm.ap[:, I2, :], pqs.pair_bf, pqs.pb, bl(Qm, I2))
                self.cp("vector", Ym.ap[:, I2, :], pys.pair, pys.pb, bl(Ym, I2))
                yield
                pzs, p1 = GB(), GB()
                for h in hs:
                    i = sl(h)
                    self.mm(pzs.q(h), Qm.ap[:, i, :], Ym.ap[:, i, :], True, True, [Qm.b[i], Ym.b[i]], pzs.pb)
                for h in hs:
                    self.mm(p1.q(h), kbgT.ap[:, h, :], Sb.ap[:, h, :], True, True, [kbgT.b[h], Sb.b[h]], p1.pb)
                yield
                self.tt("vector", TTm.ap[:, I2, :], Pm.ap[:, I2, :], pzs.pair, ALU.add, bl(Pm, I2) + pzs.pb, bl(TTm, I2))
                self.tt("vector", rr.ap[:, I2, :], vb.ap[:, I2, :], p1.pair, ALU.subtract, bl(vb, I2) + p1.pb, bl(rr, I2))
                yield
                p2 = GB()
                for h in hs:
                    i = sl(h)
                    self.mm(p2.q(h), TTm.ap[:, i, :], rr.ap[:, i, :], True, True, [TTm.b[i], rr.b[i]], p2.pb)
                yield
                self.cp("scalar", vnew.ap[:, I2, :], p2.pair, p2.pb, bl(vnew, I2))
                yield
                p3, p4 = GB(), GB()
                for h in hs:
                    i = sl(h)
                    self.mm(p3.q(h), Sb.ap[:, h, :], qdT.ap[:, h, :], True, False, [Sb.b[h], qdT.b[h]], p3.pb)
                    self.mm(p3.q(h), vnew.ap[:, i, :], attnT.ap[:, i, :], False, True, [vnew.b[i], attnT.b[i]], p3.pb)
                    self.mm(p4.q(h), kdec.ap[:, i, :], vnew.ap[:, i, :], True, True, [kdec.b[i], vnew.b[i]], p4.pb)
                yield
                self.cp("scalar", oTc.ap[:, H2, :], p3.pair, p3.pb, bl(oTc, H2))
                for h in hs:
                    self.stt(S.ap[:, h, :], S.ap[:, h, :], eg_c[:, h:h + 1], p4.q(h), ALU.mult, ALU.add,
                             [S.b[h]] + cols.b + p4.pb, [S.b[h]])
                self.cp("gpsimd", Sb.ap[:, H2, :], S.ap[:, H2, :], bl(S, H2), bl(Sb, H2))
                yield

            for g0 in range(0, 8, G):
                gens = [pipe(g0, [g0, g0 + 1]), pipe(g0, [g0 + 2, g0 + 3])]
                live = list(gens)
                while live:
                    for gen in list(live):
                        try:
                            next(gen)
                        except StopIteration:
                            live.remove(gen)
            self.chk(9)
            sabs = []
            for hf in range(2):
                sa, sb_ = self.bank(4, 8)
                src = oTc.ap[:, hf * 4:(hf + 1) * 4, :]
                oq = osq[hf]
                self.act(oq.ap.rearrange("p (h t) -> p h t", h=4), src, AF.Square, oTc.b[hf * 4:(hf + 1) * 4], oq.b)
                self.mm(sa, self.ones_b.ap, oq.ap, True, True, self.ones_b.b + oq.b, sb_)
                sabs.append((sa, sb_))
            for hf in range(2):
                o_ = ors.ap[:, hf * 512:(hf + 1) * 512]
                self.act(o_, sabs[hf][0], AF.Ln, sabs[hf][1], ors.b, bias=RMS_EPS, scale=1.0 / 128.0)
            self.act(ors.ap, ors.ap, AF.Exp, ors.b, ors.b, scale=-0.5)
            self.stt(ogt.ap.rearrange("p h t -> p (h t)"), oTc.ap.rearrange("p h t -> p (h t)"), self.vcol("dn_out_norm"),
                     ors.ap, ALU.mult, ALU.mult, oTc.b + ors.b + vecs_b, ogt.b)
            self.tt("gpsimd", xn.ap[:, :, cs], ogt.ap, zs.ap[:, :, cs], ALU.mult, ogt.b + zs.b, xn.b)
        self.P.mute = False
        self.proj(xn, "w_out", self.resid_evac())

    def build(self):
        st = self.stages
        self.phase_bufs = []
        self.prologue()
        self.phase_bufs = self.pro_bufs
        self.wstream_init()
        for t in range(self.ntiles):
            self.load_tile(t)
            if "l0mix" in st:
                self.deltanet()
            if "l0xa" in st:
                self.xattn(0)
            if "l0mlp" in st:
                self.mlp(0)
            if "l1mix" in st:
                self.conformer()
            if "l1xa" in st:
                self.xattn(1)
            if "l1mlp" in st:
                self.mlp(1)
            self.store_tile(t)
        self.jit_flush()
        fw = [(k, v[1]) for k, v in self.P.dsem.items()]
        self.P.emit(fw)


ALL_STAGES = ("l0mix", "l0xa", "l0mlp", "l1mix", "l1xa", "l1mlp")


def build_nc(ntiles=NT, stages=ALL_STAGES, final_norm=True):
    nc = bass.Bass("TRN2", target_bir_lowering=False, dynamic_dma_scratch_size=1024)
    with ExitStack() as st:
        b = Builder(nc, st, ntiles, stages, final_norm)
        b.build()
    return nc


def pack_vecs(inp):
    v = np.zeros((128, NVEC), np.float32)

    def put(name, arr):
        a = np.asarray(arr, np.float32).reshape(-1, 128).T
        v[:, VC[name]:VC[name] + a.shape[1]] = a

    put("dn_norm", inp["dn_norm"][0])
    put("cv_norm", inp["cv_norm"][0])
    put("xa_norm0", inp["xa_norm"][0])
    put("xa_norm1", inp["xa_norm"][1])
    put("mlp_norm0", inp["mlp_norm"][0])
    put("mlp_norm1", inp["mlp_norm"][1])
    put("dn_w_conv", inp["dn_w_conv"][0].reshape(-1))
    put("dn_out_norm", inp["dn_out_norm"][0])
    put("cv_b_pw1", inp["cv_b_pw1"][0])
    put("cv_b_dw", inp["cv_b_dw"][0])
    put("cv_ln_g", inp["cv_ln_g"][0])
    put("cv_ln_b", inp["cv_ln_b"][0])
    put("cv_b_pw2", inp["cv_b_pw2"][0])
    put("cv_w_dw", inp["cv_w_dw"][0].reshape(-1))
    v[0:8, VC["a_log"]] = np.asarray(inp["dn_a_log"], np.float32)[0]
    v[0:8, VC["dt_bias"]] = np.asarray(inp["dn_dt_bias"], np.float32)[0]
    return v


def make_in_maps(inp, ncores=8):
    f = lambda a: np.ascontiguousarray(np.asarray(a, np.float32))
    shared = {
        "vecs": pack_vecs(inp),
        "memg": f(inp["xa_mem_norm"]),
        "fing": f(inp["final_norm"]).reshape(1, D),
        "dn_w_in": f(inp["dn_w_in"][0]), "dn_w_out": f(inp["dn_w_out"][0]),
        "cv_w_pw1": f(inp["cv_w_pw1"][0]), "cv_w_pw2": f(inp["cv_w_pw2"][0]),
        "xa_w_q": f(inp["xa_w_q"]), "xa_w_kv": f(inp["xa_w_kv"]), "xa_w_o": f(inp["xa_w_o"]),
        "mlp_w_up": f(inp["mlp_w_up"]), "mlp_w_down": f(inp["mlp_w_down"]),
    }
    maps = []
    for c in range(ncores):
        m = dict(shared)
        m["x"] = f(inp["x"][c])
        m["mem"] = f(inp["mem"][c])
        maps.append(m)
    return maps


def kernel(**inputs):
    nc = build_nc()
    maps = make_in_maps(inputs, 8)
    res = run_bass_kernel_spmd(nc, maps, core_ids=list(range(8)))
    return np.stack([np.asarray(r["out"], np.float32) for r in res.results], axis=0)
```
